# Optimizing a Trainium2 kernel written in Bass

```python
import jax, jax.numpy as jnp
from jax import lax
import numpy as np

D_MODEL = 1024
BATCH = 16
SEQ = 4096
DEPTH = 1
DEC_BATCH = 2
DEC_SEQ = 16384
PAST_LEN = 128

HEAD_DIM = 64
N_ATTN_HEADS = 8
ATTN_WIDTH = N_ATTN_HEADS * HEAD_DIM
CONV_WIDTH = D_MODEL - ATTN_WIDTH
IN_PROJ_WIDTH = 3 * ATTN_WIDTH + 3 * CONV_WIDTH
CONV_K = 3
D_FF = 2816
ROPE_THETA = 500000.0
ROT_DIM = HEAD_DIM // 4
DILATED_BRANCHES = ((128, 1), (512, 4), (2048, 16))
Q_BLOCK = 128
NORM_EPS = 1e-6

kernel_name = "hymba_conv_dilated_macaron_encoder"


def rms_norm(x, g):
    xf = x.astype(jnp.float32)
    y = xf * lax.rsqrt(jnp.mean(xf * xf, axis=-1, keepdims=True) + NORM_EPS)
    return (y * g.astype(jnp.float32)).astype(x.dtype)


def swiglu(x, w_gate, w_up, w_down):
    return (jax.nn.silu(x @ w_gate) * (x @ w_up)) @ w_down


def partial_rope(t, pos):
    half = ROT_DIM // 2
    inv_freq = jnp.power(jnp.float32(ROPE_THETA), -jnp.arange(half, dtype=jnp.float32) * 2.0 / ROT_DIM)
    ang = pos.astype(jnp.float32)[:, None] * inv_freq[None, :]
    cos = jnp.cos(ang)[None, :, None, :]
    sin = jnp.sin(ang)[None, :, None, :]
    tf = t.astype(jnp.float32)
    t1 = tf[..., :half]
    t2 = tf[..., half:ROT_DIM]
    out = jnp.concatenate([t1 * cos - t2 * sin, t2 * cos + t1 * sin, tf[..., ROT_DIM:]], axis=-1)
    return out.astype(t.dtype)


def dilated_branch(q, k, v, window, dil):
    B, S, H, Dh = q.shape
    L = S // dil
    half = window // (2 * dil)
    nb = -(-L // Q_BLOCK)
    Lp = nb * Q_BLOCK
    span = Q_BLOCK + 2 * half

    def to_sub(t):
        return t.reshape(B, L, dil, H, Dh).transpose(0, 2, 1, 3, 4).reshape(B * dil, L, H, Dh)

    qs = jnp.pad(to_sub(q), ((0, 0), (0, Lp - L), (0, 0), (0, 0)))
    pad_kv = ((0, 0), (half, Lp - L + half), (0, 0), (0, 0))
    ks = jnp.pad(to_sub(k), pad_kv)
    vs = jnp.pad(to_sub(v), pad_kv)

    blk = jnp.arange(nb)[:, None] * Q_BLOCK
    kidx = blk + jnp.arange(span)[None, :]
    kb = ks[:, kidx].astype(jnp.float32)
    vb = vs[:, kidx].astype(jnp.float32)
    qb = qs.reshape(B * dil, nb, Q_BLOCK, H, Dh).astype(jnp.float32)

    s = jnp.einsum('bnqhd,bnkhd->bnhqk', qb, kb) * (Dh ** -0.5)
    kpos = (kidx - half)[:, None, :]
    qpos = (blk + jnp.arange(Q_BLOCK)[None, :])[:, :, None]
    rel = kpos - qpos
    valid = (jnp.abs(rel) <= half) & (((kpos >= 0) & (kpos < L)) | (rel == 0))
    s = jnp.where(valid[None, :, None], s, -jnp.inf)
    m = jnp.max(s, axis=-1, keepdims=True)
    p = jnp.exp(s - m)
    den = jnp.sum(p, axis=-1, keepdims=True)
    o = jnp.einsum('bnhqk,bnkhd->bnqhd', p, vb) / jnp.transpose(den, (0, 1, 3, 2, 4))
    lse = jnp.transpose((m + jnp.log(den))[..., 0], (0, 1, 3, 2))

    def from_sub(t):
        rest = t.shape[3:]
        t = t.reshape((B * dil, Lp) + rest)[:, :L]
        t = t.reshape((B, dil, L) + rest)
        t = jnp.swapaxes(t, 1, 2)
        return t.reshape((B, S) + rest)

    return from_sub(o), from_sub(lse)


def dilated_mixture_attention(q, k, v):
    outs, lses = [], []
    for window, dil in DILATED_BRANCHES:
        o, l = dilated_branch(q, k, v, window, dil)
        outs.append(o)
        lses.append(l)
    w = jax.nn.softmax(jnp.stack(lses, axis=0), axis=0)
    o = jnp.sum(w[..., None] * jnp.stack(outs, axis=0), axis=0)
    return o.astype(q.dtype)


def short_conv_mixer(b, c, v, conv_w):
    u = c * v
    S = u.shape[1]
    up = jnp.pad(u, ((0, 0), (1, 1), (0, 0)))
    conv = up[:, 0:S] * conv_w[0] + up[:, 1:S + 1] * conv_w[1] + up[:, 2:S + 2] * conv_w[2]
    return b * conv


def encoder_layer(x, ffn1_pre_g, ffn1_w_gate, ffn1_w_up, ffn1_w_down, ffn1_post_g,
                  mix_pre_g, w_in, conv_w, attn_out_g, conv_out_g, w_out, mix_post_g,
                  ffn2_pre_g, ffn2_w_gate, ffn2_w_up, ffn2_w_down, ffn2_post_g):
    B, S, _ = x.shape
    h = x + 0.5 * rms_norm(swiglu(rms_norm(x, ffn1_pre_g), ffn1_w_gate, ffn1_w_up, ffn1_w_down), ffn1_post_g)
    u = rms_norm(h, mix_pre_g)
    z = u @ w_in
    q, k, v, cb, cc, cv = jnp.split(z, 6, axis=-1)
    pos = jnp.arange(S)
    q = partial_rope(q.reshape(B, S, N_ATTN_HEADS, HEAD_DIM), pos)
    k = partial_rope(k.reshape(B, S, N_ATTN_HEADS, HEAD_DIM), pos)
    v = v.reshape(B, S, N_ATTN_HEADS, HEAD_DIM)
    attn = dilated_mixture_attention(q, k, v).reshape(B, S, ATTN_WIDTH)
    conv = short_conv_mixer(cb, cc, cv, conv_w)
    mixed = jnp.concatenate([rms_norm(attn, attn_out_g), rms_norm(conv, conv_out_g)], axis=-1) @ w_out
    h = h + rms_norm(mixed, mix_post_g)
    h = h + 0.5 * rms_norm(swiglu(rms_norm(h, ffn2_pre_g), ffn2_w_gate, ffn2_w_up, ffn2_w_down), ffn2_post_g)
    return h


def setup_inputs(seed: int = 0) -> dict:
    key = jax.random.key(seed)
    ks = jax.random.split(key, 20)
    f32 = jnp.float32

    def nrm(k, shape, scale):
        return jax.random.normal(k, shape, f32) * scale

    def gain(k, n):
        return 1.0 + 0.02 * jax.random.normal(k, (DEPTH, n), f32)

    return {
        "x_prompt": jax.random.normal(ks[0], (BATCH, SEQ, D_MODEL), f32),
        "x_sample": jax.random.normal(ks[1], (DEC_BATCH, DEC_SEQ, D_MODEL), f32),
        "ffn1_pre_g": gain(ks[2], D_MODEL),
        "ffn1_w_gate": nrm(ks[3], (DEPTH, D_MODEL, D_FF), D_MODEL ** -0.5),
        "ffn1_w_up": nrm(ks[4], (DEPTH, D_MODEL, D_FF), D_MODEL ** -0.5),
        "ffn1_w_down": nrm(ks[5], (DEPTH, D_FF, D_MODEL), D_FF ** -0.5),
        "ffn1_post_g": gain(ks[6], D_MODEL),
        "mix_pre_g": gain(ks[7], D_MODEL),
        "w_in": nrm(ks[8], (DEPTH, D_MODEL, IN_PROJ_WIDTH), D_MODEL ** -0.5),
        "conv_w": nrm(ks[9], (DEPTH, CONV_K, CONV_WIDTH), CONV_K ** -0.5),
        "attn_out_g": gain(ks[10], ATTN_WIDTH),
        "conv_out_g": gain(ks[11], CONV_WIDTH),
        "w_out": nrm(ks[12], (DEPTH, D_MODEL, D_MODEL), D_MODEL ** -0.5),
        "mix_post_g": gain(ks[13], D_MODEL),
        "ffn2_pre_g": gain(ks[14], D_MODEL),
        "ffn2_w_gate": nrm(ks[15], (DEPTH, D_MODEL, D_FF), D_MODEL ** -0.5),
        "ffn2_w_up": nrm(ks[16], (DEPTH, D_MODEL, D_FF), D_MODEL ** -0.5),
        "ffn2_w_down": nrm(ks[17], (DEPTH, D_FF, D_MODEL), D_FF ** -0.5),
        "ffn2_post_g": gain(ks[18], D_MODEL),
    }


def reference(x_prompt, x_sample, ffn1_pre_g, ffn1_w_gate, ffn1_w_up, ffn1_w_down, ffn1_post_g,
              mix_pre_g, w_in, conv_w, attn_out_g, conv_out_g, w_out, mix_post_g,
              ffn2_pre_g, ffn2_w_gate, ffn2_w_up, ffn2_w_down, ffn2_post_g):
    y_prompt = x_prompt
    y_sample = x_sample
    for l in range(DEPTH):
        p = (ffn1_pre_g[l], ffn1_w_gate[l], ffn1_w_up[l], ffn1_w_down[l], ffn1_post_g[l],
             mix_pre_g[l], w_in[l], conv_w[l], attn_out_g[l], conv_out_g[l], w_out[l], mix_post_g[l],
             ffn2_pre_g[l], ffn2_w_gate[l], ffn2_w_up[l], ffn2_w_down[l], ffn2_post_g[l])
        y_prompt = encoder_layer(y_prompt, *p)
        y_sample = encoder_layer(y_sample, *p)
    return (y_prompt, y_sample)
```

```python
import contextlib
import numpy as np
import concourse.bass as bass
import concourse.mybir as mybir
from concourse.bass_utils import run_bass_kernel_spmd

F32 = mybir.dt.float32
BF16 = mybir.dt.bfloat16
AF = mybir.ActivationFunctionType
ALU = mybir.AluOpType

D = 1024
DFF = 2816
NFC = DFF // 128
T = 512
EPS = 1e-6
MW = 2944
NSLOT = 6
SLOTW = 3072
ROPE_THETA = 500000.0


class Buf:
    __slots__ = ("name", "w", "rs")

    def __init__(self, name):
        self.name = name
        self.w = None
        self.rs = []


class Q:
    def __init__(self, name, sem, is_pe=False):
        self.name = name
        self.sem = sem
        self.count = 0
        self.ops = []
        self.waited = {}
        self.is_pe = is_pe

    def wait_ev(self, ev):
        if ev is None:
            return
        sem, val = ev
        if sem is self.sem and self.is_pe:
            return
        k = id(sem)
        if self.waited.get(k, 0) >= val:
            return
        self.waited[k] = val
        self.ops.append(("w", sem, val))

    def sync(self, reads, writes):
        for b in reads:
            self.wait_ev(b.w)
        for b in writes:
            if b.w is not None and b.w[0] is not self.sem:
                self.wait_ev(b.w)
            for r in b.rs:
                if r[0] is not self.sem:
                    self.wait_ev(r)

    def emit(self, fn, reads=(), writes=()):
        self.sync(reads, writes)
        self.count += 1
        ev = (self.sem, self.count)
        self.ops.append(("i", fn, self.sem, 1))
        for b in reads:
            b.rs.append(ev)
        for b in writes:
            b.w = ev
            b.rs = []
        return ev

    def emit_group(self, fns, reads=(), writes=()):
        self.sync(reads, writes)
        for fn in fns[:-1]:
            self.ops.append(("n", fn))
        self.count += 1
        ev = (self.sem, self.count)
        self.ops.append(("i", fns[-1], self.sem, 1))
        for b in reads:
            b.rs.append(ev)
        for b in writes:
            b.w = ev
            b.rs = []
        return ev


class Chan:
    def __init__(self, sem):
        self.sem = sem
        self.count = 0

    def dma(self, q, out, in_, reads=(), writes=(), chain=True, more=()):
        q.sync(reads, writes)
        if chain and self.count:
            q.wait_ev((self.sem, self.count))
        q.ops.append(("d", out, in_, self.sem))
        self.count += 16
        for (o2, i2) in more:
            q.ops.append(("d", o2, i2, self.sem))
            self.count += 16
        ev = (self.sem, self.count)
        for b in reads:
            b.rs.append(ev)
        for b in writes:
            b.w = ev
            b.rs = []
        return ev


def _replay(q, eng):
    for op in q.ops:
        k = op[0]
        if k == "w":
            eng.wait_ge(op[1], op[2])
        elif k == "i":
            op[1](eng).then_inc(op[2], op[3])
        elif k == "n":
            op[1](eng)
        elif k == "d":
            eng.dma_start(out=op[1], in_=op[2]).then_inc(op[3], 16)


DEBUG = False
_DBG = {}


def _build(specs):
    nA = [s["nc"] + s["hl"] + s["hr"] for s in specs]
    nC = [s["nc"] for s in specs]
    offA = [sum(nA[:i]) for i in range(len(specs))]
    offC = [sum(nC[:i]) for i in range(len(specs))]
    NTA = sum(nA)
    NTC = sum(nC)
    NA = NTA * T
    NCT = NTC * T

    nc = bass.Bass("TRN2", target_bir_lowering=False)

    def din(name, shape, dt=F32):
        return nc.dram_tensor(name, shape, dt, kind="ExternalInput").ap()

    def dscr(name, shape, dt):
        if DEBUG and name.endswith("_s"):
            return nc.dram_tensor(name, shape, dt, kind="ExternalOutput").ap()
        return nc.dram_tensor(name, shape, dt, kind="Internal").ap()

    xT = din("xT", [D, NA])
    cs_d = din("cs", [NTA, 128, 2 * T])
    validx_d = din("validx", [128, NTA * 4 * 8])
    smalls_d = din("smalls", [128, 72])
    mask_d = din("maskT", [128, 2 * MW])
    gu_d = [din("gu1", [NFC * 128, 2048]), din("gu2", [NFC * 128, 2048])]
    dn_d = [din("dn1", [8 * 128, DFF]), din("dn2", [8 * 128, DFF])]
    win_d = din("win", [20 * 128, 1024])
    wv_d = din("wv", [128, 4096])
    woa_d = din("woa", [4 * 64, 2048])
    woc_d = din("woc", [4 * 128, 1024])
    yT = nc.dram_tensor("yT", [D, NCT], F32, kind="ExternalOutput").ap()

    gu_b = [dscr("gu1b", [NFC * 128, 2048], BF16), dscr("gu2b", [NFC * 128, 2048], BF16)]
    dn_b = [dscr("dn1b", [8 * 128, DFF], BF16), dscr("dn2b", [8 * 128, DFF], BF16)]
    win_b = dscr("winb", [20 * 128, 1024], BF16)
    wv_b = dscr("wvb", [128, 4096], BF16)
    woa_b = dscr("woab", [4 * 64, 2048], BF16)
    woc_b = dscr("wocb", [4 * 128, 1024], BF16)

    h_s = dscr("h_s", [128, 8, NCT], F32)
    q_s = dscr("q_s", [128, 4, NCT], BF16)
    k_s = dscr("k_s", [128, 4, NA], BF16)
    v_s = dscr("v_s", [128, NTA * 4, 520], BF16)
    cb_s = dscr("cb_s", [128, 4, NCT], F32)
    ccv_s = dscr("ccv_s", [128, 4, NA], F32)
    a_s = dscr("a_s", [64, 8, NCT], F32)

    xT3 = xT.rearrange("(k p) n -> p k n", p=128)
    yT3 = yT.rearrange("(k p) n -> p k n", p=128)

    es = contextlib.ExitStack()
    with es:
        def sb(name, shape, dt):
            return es.enter_context(nc.sbuf_tensor(name, shape, dt))

        def sem(name):
            return es.enter_context(nc.semaphore(name))

        PE = Q("pe", sem("s_pe"), is_pe=True)
        ACT = Q("act", sem("s_act"))
        DVE = Q("dve", sem("s_dve"))
        POOL = Q("pool", sem("s_pool"))
        SP = Q("sp", sem("s_sp"))

        ALLQ = (PE, ACT, DVE, POOL, SP)
        chans = []

        def chan(name):
            c = Chan(sem(name))
            chans.append(c)
            return c

        maskT = sb("maskT_sb", [128, 2, MW], BF16)
        smalls = sb("smalls_sb", [128, 72], F32)
        ghalf = sb("ghalf", [128, 16], F32)
        onesm1024 = sb("onesm1024", [128, 128], BF16)
        onesm512 = sb("onesm512", [128, 128], BF16)
        ones32 = sb("ones32", [128, 64], F32)
        wslot = [sb(f"wslot{i}", [128, SLOTW], BF16) for i in range(NSLOT)]
        xb = [sb(f"xb{i}", [128, 8, T], F32) for i in range(2)]
        bank03 = es.enter_context(nc.psum_tensor("bank05", [128, 6 * T], F32))
        banks = [bank03[:, i * T:(i + 1) * T] for i in range(6)]
        banks += [es.enter_context(nc.psum_tensor(f"bank{i}", [128, T], F32))[:] for i in range(6, 8)]

        B = {}

        def bf(name):
            if name not in B:
                B[name] = Buf(name)
            return B[name]

        bank_b = [bf(f"bank{i}") for i in range(8)]

        ch_const = chan("c_const")
        ch_cast = [chan(f"c_cast{i}") for i in range(5)]
        ch_slot = [chan(f"c_slot{i}") for i in range(NSLOT)]
        ch_x = [chan(f"c_x{i}") for i in range(2)]
        ch_cs = chan("c_cs")
        ch_st = {n: chan("c_st_" + n) for n in ["h", "q", "k", "v", "cb", "ccv", "a", "y"]}
        ch_ld = {n: chan("c_ld_" + n) for n in ["q0", "q1", "a", "cb", "ccv"]}
        ch_ring = [chan(f"c_ring{i}") for i in range(5)]

        blk_no = [0]

        def flush_block():
            with nc.Block() as block:
                @block.tensor
                def _(e):
                    _replay(PE, e)

                @block.scalar
                def _(e):
                    _replay(ACT, e)

                @block.vector
                def _(e):
                    _replay(DVE, e)

                @block.gpsimd
                def _(e):
                    _replay(POOL, e)

                @block.sync
                def _(e):
                    _replay(SP, e)
            for q in ALLQ:
                q.ops = []
            blk_no[0] += 1

        def barrier():
            evs = [(q.sem, q.count) for q in ALLQ if q.count] + [(c.sem, c.count) for c in chans if c.count]
            for q in ALLQ:
                for ev in evs:
                    q.wait_ev(ev)
            for b in B.values():
                b.w = None
                b.rs = []

        b_smalls, b_mask, b_valid, b_ones = bf("smalls"), bf("mask"), bf("valid"), bf("ones")
        ch_const.dma(POOL, smalls[:], smalls_d, writes=[b_smalls], chain=False,
                     more=[(maskT[:].rearrange("p a m -> p (a m)"), mask_d)])
        b_mask.w = b_smalls.w
        POOL.emit(lambda e: e.memset(onesm1024[:], 1.0 / 1024.0), writes=[b_ones])
        POOL.emit(lambda e: e.memset(onesm512[:], 1.0 / 512.0), writes=[b_ones])
        POOL.emit(lambda e: e.memset(ones32[:], 1.0), writes=[b_ones])
        b_ghalf = bf("ghalf")
        DVE.emit(lambda e: e.tensor_scalar_mul(out=ghalf[:, 0:8], in0=smalls[:, 8:16], scalar1=0.5),
                 reads=[b_smalls], writes=[b_ghalf])
        DVE.emit(lambda e: e.tensor_scalar_mul(out=ghalf[:, 8:16], in0=smalls[:, 40:48], scalar1=0.5),
                 reads=[b_smalls], writes=[b_ghalf])

        def cast(grp, dst, src, rows, step):
            for r0 in range(0, rows, step):
                r1 = min(rows, r0 + step)
                ch_cast[grp].dma(POOL, dst[r0:r1, :], src[r0:r1, :], chain=False)

        plan = []
        for si, s in enumerate(specs):
            n_ = nA[si]
            cen = [s["hl"] <= ta < s["hl"] + s["nc"] for ta in range(n_)]
            plan += [("gu", 0, fc) for fc in range(NFC)]
            for ta in range(n_):
                plan += [("dn", 0, dc) for dc in range(8)]
                if ta + 1 < n_:
                    plan += [("gu", 0, fc) for fc in range(NFC)]
                for u in range(10):
                    if not cen[ta] and (u < 2 or u >= 8):
                        continue
                    plan.append(("win", u))
                plan.append(("wv", 0))
                plan.append(("wv", 1))
            ncn_ = nC[si]
            plan += [("wo", u) for u in range(4)]
            plan += [("gu", 1, fc) for fc in range(NFC)]
            for tc in range(ncn_):
                if tc + 1 < ncn_:
                    plan += [("wo", u) for u in range(4)]
                plan += [("dn", 1, dc) for dc in range(8)]
                if tc + 1 < ncn_:
                    plan += [("gu", 1, fc) for fc in range(NFC)]

        wgrp_of = {"gu0": 0, "dn0": 1, "win": 2, "wv": 2, "wo": 2, "gu1": 3, "dn1": 4}
        slot_b = [bf(f"wslot{i}") for i in range(NSLOT)]
        ws = {"issued": 0, "next": 0, "grp_waited": set()}

        def ws_issue(i):
            unit = plan[i]
            sl = i % NSLOT
            t = wslot[sl]
            kind = unit[0]
            g = wgrp_of[kind + (str(unit[1]) if kind in ("gu", "dn") else "")]
            if g not in ws["grp_waited"]:
                ws["grp_waited"].add(g)
                SP.wait_ev((ch_cast[g].sem, ch_cast[g].count))
            ch = ch_slot[sl]
            wr = [slot_b[sl]]
            if kind == "gu":
                _, f, fc = unit
                ch.dma(SP, t[:, 0:2048], gu_b[f][fc * 128:(fc + 1) * 128, :], writes=wr, chain=False)
            elif kind == "dn":
                _, f, dc = unit
                ch.dma(SP, t[:, 0:DFF], dn_b[f][dc * 128:(dc + 1) * 128, :], writes=wr, chain=False)
            elif kind == "win":
                u = unit[1]
                oc = 2 * u
                ch.dma(SP, t[:, 0:1024], win_b[oc * 128:(oc + 1) * 128, :], writes=wr, chain=False,
                       more=[(t[:, 1024:2048], win_b[(oc + 1) * 128:(oc + 2) * 128, :])])
            elif kind == "wv":
                hv = unit[1]
                ch.dma(SP, t[:, 0:2048], wv_b[:, hv * 2048:(hv + 1) * 2048], writes=wr, chain=False)
            elif kind == "wo":
                u = unit[1]
                ch.dma(SP, t[0:64, 0:2048], woa_b[u * 64:(u + 1) * 64, :], writes=wr, chain=False,
                       more=[(t[:, 2048:3072], woc_b[u * 128:(u + 1) * 128, :])])

        def ws_get(expect):
            i = ws["next"]
            assert plan[i] == expect, (plan[i], expect)
            while ws["issued"] < min(len(plan), i + NSLOT - 1):
                ws_issue(ws["issued"])
                ws["issued"] += 1
            ws["next"] += 1
            sl = i % NSLOT
            return wslot[sl], slot_b[sl]

        rot = {}

        def nxt(key, n):
            v = rot.get(key, 0)
            rot[key] = v + 1
            return v % n

        def mm(out, lhsT, rhs, start, stop):
            return lambda e: e.matmul(out, lhsT=lhsT, rhs=rhs, start=start, stop=stop)

        def act(out, in_, func, scale=1.0, bias=0.0):
            return lambda e: e.activation(out=out, in_=in_, func=func, bias=bias, scale=scale)

        def xbufs_of(xsl):
            return [bf(f"xb{xsl}_{kc}") for kc in range(8)]

        class FFNSet:
            def __init__(self, pes, tag):
                def a(name, shape, dt):
                    return pes.enter_context(nc.sbuf_tensor(f"{name}_{tag}", shape, dt))
                self.xn = a("xn", [128, 8, T], BF16)
                self.hid = a("hid", [128, NFC, T], BF16)
                self.ysb = a("ysb", [128, 8, T], F32)
                self.sqb = [a(f"sqb{i}", [128, T], BF16) for i in range(3)]
                self.dsq = [a(f"dsq{i}", [128, T], BF16) for i in range(2)]
                self.sgb = [a(f"sgb{i}", [128, T], F32) for i in range(2)]
                self.rstd = [a(f"rstd{i}", [128, T], F32) for i in range(2)]
                self.t1b = [a(f"t1b{i}", [128, T], F32) for i in range(2)]

        def rms_rstd(F, chunk_aps, chunk_bufs, ones_ap, bank_i, npart, rs_i):
            n = len(chunk_aps)
            pend = []
            for i, (ap, b) in enumerate(zip(chunk_aps, chunk_bufs)):
                si_ = nxt("sq", 3)
                k = ap.shape[0]
                sq_ap = F.sqb[si_][0:k, :]
                ACT.emit(act(sq_ap, ap, AF.Square), reads=[b], writes=[bf(f"sqb{si_}")])
                if len(pend) >= 2:
                    pend.pop(0)()
                pend.append(lambda i=i, k=k, sq_ap=sq_ap, si_=si_: PE.emit_group(
                    [mm(banks[bank_i][0:npart, :], ones_ap[0:k, 0:npart], sq_ap, i == 0, i == n - 1)],
                    reads=[bf(f"sqb{si_}"), b_ones], writes=[bank_b[bank_i]]))
            while pend:
                pend.pop(0)()
            rb = bf(f"rstd{rs_i}")
            ACT.emit(act(F.rstd[rs_i][0:npart, :], banks[bank_i][0:npart, :], AF.Sqrt, bias=EPS),
                     reads=[bank_b[bank_i]], writes=[rb])
            DVE.emit(lambda e: e.reciprocal(out=F.rstd[rs_i][0:npart, :], in_=F.rstd[rs_i][0:npart, :]),
                     reads=[rb], writes=[rb])
            return rb

        def norm_parts(F, xsl, gcol0, xn_t, xn_name):
            xbufs = xbufs_of(xsl)
            st = {}

            def head():
                st["rs"] = nxt("rs", 2)
                st["rb"] = rms_rstd(F, [xb[xsl][:, kc, :] for kc in range(8)], xbufs, onesm1024, 7, 128, st["rs"])

            def part(k0):
                rs_i, rb = st["rs"], st["rb"]
                for kc in (k0, k0 + 1):
                    DVE.emit(lambda e, kc=kc: e.scalar_tensor_tensor(
                        out=xn_t[:, kc, :], in0=xb[xsl][:, kc, :], scalar=smalls[:, gcol0 + kc:gcol0 + kc + 1],
                        in1=F.rstd[rs_i][:], op0=ALU.mult, op1=ALU.mult),
                        reads=[xbufs[kc], rb, b_smalls], writes=[bf(f"{xn_name}{kc}")])
            return [head] + [lambda k0=k0: part(k0) for k0 in (0, 2, 4, 6)]

        def norm_to(F, xsl, gcol0, xn_t, xn_name):
            for th in norm_parts(F, xsl, gcol0, xn_t, xn_name):
                th()

        def post_parts(F, xsl, bank_i, gt, g0, gbuf, ysrc, yname):
            st = {}

            def head():
                rs_i = nxt("rs", 2)
                rb = bf(f"rstd{rs_i}")
                st["rs"], st["rb"] = rs_i, rb
                ACT.emit(act(F.rstd[rs_i][:], banks[bank_i][:], AF.Sqrt, bias=EPS), reads=[bank_b[bank_i]],
                         writes=[rb])
                DVE.emit(lambda e: e.reciprocal(out=F.rstd[rs_i][:], in_=F.rstd[rs_i][:]), reads=[rb], writes=[rb])

            def part(d0):
                rs_i, rb = st["rs"], st["rb"]
                for dc in (d0, d0 + 1):
                    ti = nxt("t1", 2)
                    DVE.emit(lambda e, dc=dc, ti=ti: e.scalar_tensor_tensor(
                        out=F.t1b[ti][:], in0=ysrc[:, dc, :], scalar=gt[:, g0 + dc:g0 + dc + 1],
                        in1=F.rstd[rs_i][:], op0=ALU.mult, op1=ALU.mult),
                        reads=[bf(f"{yname}{dc}"), rb, gbuf], writes=[bf(f"t1b{ti}")])
                    eng = POOL if dc % 2 == 0 else DVE
                    eng.emit(lambda e, dc=dc, ti=ti: e.tensor_tensor(
                        out=xb[xsl][:, dc, :], in0=xb[xsl][:, dc, :], in1=F.t1b[ti][:], op=ALU.add),
                        reads=[bf(f"t1b{ti}"), bf(f"xb{xsl}_{dc}")], writes=[bf(f"xb{xsl}_{dc}")])
            return [head] + [lambda d0=d0: part(d0) for d0 in (0, 2, 4, 6)]

        def post_residual(F, xsl, bank_i, gt, g0, gbuf, ysrc, yname):
            for th in post_parts(F, xsl, bank_i, gt, g0, gbuf, ysrc, yname):
                th()

        def spread(hooks, start, thunks, step=1):
            for i, th in enumerate(thunks):
                hooks.setdefault(start + i * step, []).append(th)
            return hooks

        def run_hooks(hooks, i):
            for th in hooks.get(i, ()):
                th()

        def gu_phase(F, f, hooks):
            xnb = [bf(f"xnA{kc}") for kc in range(8)]
            for fc in range(NFC):
                wt, wb = ws_get(("gu", f, fc))
                gi = nxt("gbank", 2)
                gb, ub = gi, 2 + gi
                PE.emit_group([mm(banks[gb][:], wt[:, kc * 128:(kc + 1) * 128], F.xn[:, kc, :], kc == 0, kc == 7)
                               for kc in range(8)], reads=[wb] + xnb, writes=[bank_b[gb]])
                PE.emit_group([mm(banks[ub][:], wt[:, 1024 + kc * 128:1024 + (kc + 1) * 128], F.xn[:, kc, :],
                                  kc == 0, kc == 7) for kc in range(8)], reads=[wb] + xnb, writes=[bank_b[ub]])
                sgi = nxt("sg", 2)
                ACT.emit(act(F.sgb[sgi][:], banks[gb][:], AF.Silu), reads=[bank_b[gb]], writes=[bf(f"sgb{sgi}")])
                DVE.emit(lambda e, fc=fc, sgi=sgi, ub=ub: e.tensor_tensor(
                    out=F.hid[:, fc, :], in0=F.sgb[sgi][:], in1=banks[ub][:], op=ALU.mult),
                    reads=[bf(f"sgb{sgi}"), bank_b[ub]], writes=[bf(f"hid{fc}")])
                run_hooks(hooks, fc)

        def dn_phase(F, f, hooks):
            hb = [bf(f"hid{fc}") for fc in range(NFC)]
            pend = []
            for dc in range(8):
                wt, wb = ws_get(("dn", f, dc))
                yb = 4 + nxt("ybank", 2)
                PE.emit_group([mm(banks[yb][:], wt[:, fc * 128:(fc + 1) * 128], F.hid[:, fc, :], fc == 0, fc == NFC - 1)
                               for fc in range(NFC)], reads=[wb] + hb, writes=[bank_b[yb]])
                ACT.emit(act(F.ysb[:, dc, :], banks[yb][:], AF.Copy), reads=[bank_b[yb]], writes=[bf(f"ysb{dc}")])
                si_ = nxt("dsq", 2)
                ACT.emit(act(F.dsq[si_][:], banks[yb][:], AF.Square), reads=[bank_b[yb]], writes=[bf(f"dsq{si_}")])
                if pend:
                    pend.pop(0)()
                pend.append(lambda si_=si_, dc=dc: PE.emit_group(
                    [mm(banks[6][:], onesm1024[:], F.dsq[si_][:], dc == 0, dc == 7)],
                    reads=[bf(f"dsq{si_}"), b_ones], writes=[bank_b[6]]))
                run_hooks(hooks, dc)
            while pend:
                pend.pop(0)()

        def load_x(si, ta, xsl):
            g = offA[si] + ta
            ch_x[xsl].dma(POOL, xb[xsl][:], xT3[:, :, g * T:(g + 1) * T], writes=xbufs_of(xsl))

        load_x(0, 0, 0)
        cast(0, gu_b[0], gu_d[0], NFC * 128, 512)
        cast(1, dn_b[0], dn_d[0], 8 * 128, 256)
        cast(2, win_b, win_d, 20 * 128, 1024)
        cast(2, wv_b, wv_d, 128, 128)
        cast(2, woa_b, woa_d, 256, 256)
        cast(2, woc_b, woc_d, 512, 512)
        cast(3, gu_b[1], gu_d[1], NFC * 128, 512)
        cast(4, dn_b[1], dn_d[1], 8 * 128, 256)

        xsl_state = {"a": 0}
        out_events = []

        for si, s in enumerate(specs):
            hl, ncn = s["hl"], s["nc"]
            with contextlib.ExitStack() as pes:
                F = FFNSet(pes, f"A{si}")

                def pa(name, shape, dt):
                    return pes.enter_context(nc.sbuf_tensor(f"{name}_A{si}", shape, dt))
                xnU = pa("xnU", [128, 8, T], BF16)
                validx = pa("validx", [128, nA[si] * 4 * 8], F32)
                ch_const.dma(POOL, validx[:], validx_d[:, offA[si] * 32:(offA[si] + nA[si]) * 32],
                             writes=[b_valid])
                t2b = [pa(f"t2b{i}", [128, T], F32) for i in range(2)]
                zsw = [pa(f"zsw{i}", [128, T], F32) for i in range(2)]
                zf = [pa(f"zf{i}", [128, T], F32) for i in range(2)]
                for i in range(2):
                    POOL.emit(lambda e, i=i: e.memset(zsw[i][:], 0.0), writes=[bf(f"zsw{i}")])
                csb = pa("csb", [128, 2 * T], F32)
                qT = pa("qT", [128, 4, T], BF16)
                kT = pa("kT", [128, 4, T], BF16)
                vaug = pa("vaug", [128, 4, 520], BF16)
                cbT = pa("cbT", [128, 4, T], F32)
                ccvT = pa("ccvT", [128, 4, T], F32)
                nAs = nA[si]
                s0 = xsl_state["a"]
                b_cs = bf("csb")

                def slotA(t):
                    return (s0 + t) % 2

                def pn1(t):
                    norm_to(F, slotA(t), 0, F.xn, "xnA")

                def pnm_and_prefetch(t):
                    norm_to(F, slotA(t), 16, xnU, "xnU")
                    if t + 2 < nAs:
                        load_x(si, t + 2, slotA(t + 2))

                def proj_v_store(t):
                    central = hl <= t < hl + ncn
                    g = offA[si] + t
                    gc = offC[si] + (t - hl)
                    xnb = [bf(f"xnU{kc}") for kc in range(8)]
                    for u in range(10):
                        if not central and (u < 2 or u >= 8):
                            continue
                        wt, wb = ws_get(("win", u))
                        pi = nxt("zbank", 2)
                        bA, bB = pi, 2 + pi
                        PE.emit_group([mm(banks[bA][:], wt[:, kc * 128:(kc + 1) * 128], xnU[:, kc, :],
                                          kc == 0, kc == 7) for kc in range(8)],
                                      reads=[wb] + xnb, writes=[bank_b[bA]])
                        PE.emit_group([mm(banks[bB][:], wt[:, 1024 + kc * 128:1024 + (kc + 1) * 128], xnU[:, kc, :],
                                          kc == 0, kc == 7) for kc in range(8)],
                                      reads=[wb] + xnb, writes=[bank_b[bB]])
                        if u < 4:
                            dst, dname = (qT, "qT") if u < 2 else (kT, "kT")
                            for ci_, bX in enumerate((bA, bB)):
                                j = 2 * (u % 2) + ci_
                                zi = nxt("zsw", 2)
                                zb_ = bf(f"zsw{zi}")
                                zfb = bf(f"zf{zi}")
                                ACT.emit(act(zf[zi][:], banks[bX][:], AF.Copy), reads=[bank_b[bX]], writes=[zfb])
                                for (o0, i0_) in ((0, 32), (32, 0), (64, 96), (96, 64)):
                                    ACT.emit(act(zsw[zi][o0:o0 + 8, :], zf[zi][i0_:i0_ + 8, :], AF.Copy),
                                             reads=[zfb], writes=[zb_])
                                ti = nxt("t1", 2)
                                t2i = nxt("t2", 2)
                                DVE.emit(lambda e, ti=ti, zi=zi: e.tensor_tensor(
                                    out=F.t1b[ti][:], in0=zf[zi][:], in1=csb[:, 0:T], op=ALU.mult),
                                    reads=[zfb, b_cs], writes=[bf(f"t1b{ti}")])
                                DVE.emit(lambda e, t2i=t2i, zi=zi: e.tensor_tensor(
                                    out=t2b[t2i][:], in0=zsw[zi][:], in1=csb[:, T:2 * T], op=ALU.mult),
                                    reads=[zb_, b_cs], writes=[bf(f"t2b{t2i}")])
                                POOL.emit(lambda e, ti=ti, t2i=t2i, dst=dst, j=j: e.tensor_tensor(
                                    out=dst[:, j, :], in0=F.t1b[ti][:], in1=t2b[t2i][:], op=ALU.add),
                                    reads=[bf(f"t1b{ti}"), bf(f"t2b{t2i}")], writes=[bf(dname)])
                        elif u < 8:
                            j = u - 4
                            ti = nxt("t1", 2)
                            ACT.emit(act(F.t1b[ti][:], banks[bA][:], AF.Copy), reads=[bank_b[bA]],
                                     writes=[bf(f"t1b{ti}")])
                            DVE.emit(lambda e, ti=ti, bB=bB, j=j: e.tensor_tensor(
                                out=ccvT[:, j, :], in0=F.t1b[ti][:], in1=banks[bB][:], op=ALU.mult),
                                reads=[bf(f"t1b{ti}"), bank_b[bB]], writes=[bf("ccvT")])
                        else:
                            j0 = 2 * (u - 8)
                            ACT.emit(act(cbT[:, j0, :], banks[bA][:], AF.Copy), reads=[bank_b[bA]],
                                     writes=[bf("cbT")])
                            ACT.emit(act(cbT[:, j0 + 1, :], banks[bB][:], AF.Copy), reads=[bank_b[bB]],
                                     writes=[bf("cbT")])
                    wt0, wb0 = ws_get(("wv", 0))
                    wt1, wb1 = ws_get(("wv", 1))
                    for blk in range(4):
                        vb = 4 + nxt("ybank", 2)
                        fns = []
                        for kc in range(8):
                            wt = wt0 if kc < 4 else wt1
                            kl = kc % 4
                            fns.append(mm(banks[vb][:], xnU[:, kc, blk * 128:(blk + 1) * 128],
                                          wt[:, kl * 512:(kl + 1) * 512], kc == 0, kc == 7))
                        PE.emit_group(fns, reads=[wb0, wb1] + xnb, writes=[bank_b[vb]])
                        vview = vaug[:, blk, :].rearrange("p (h d) -> p h d", d=65)
                        ACT.emit(lambda e, vb=vb, vview=vview: e.activation(
                            out=vview[:, :, 0:64], in_=banks[vb][:].rearrange("p (h d) -> p h d", d=64),
                            func=AF.Copy), reads=[bank_b[vb]], writes=[bf("vaug")])
                        vx = validx[:, (t * 4 + blk) * 8:(t * 4 + blk + 1) * 8].rearrange("p (h o) -> p h o", o=1)
                        DVE.emit(lambda e, vview=vview, vx=vx: e.tensor_copy(out=vview[:, :, 64:65], in_=vx),
                                 reads=[b_valid], writes=[bf("vaug")])
                    ch_st["k"].dma(POOL, k_s[:, :, g * T:(g + 1) * T], kT[:], reads=[bf("kT")],
                                   writes=[bf(f"k_s{g}")])
                    ch_st["v"].dma(POOL, v_s[:, g * 4:(g + 1) * 4, :], vaug[:], reads=[bf("vaug")],
                                   writes=[bf(f"v_s{g}")])
                    ch_st["ccv"].dma(POOL, ccv_s[:, :, g * T:(g + 1) * T], ccvT[:], reads=[bf("ccvT")],
                                     writes=[bf(f"ccv_s{g}")])
                    if central:
                        ch_st["q"].dma(POOL, q_s[:, :, gc * T:(gc + 1) * T], qT[:], reads=[bf("qT")],
                                       writes=[bf(f"q_s{gc}")])
                        ch_st["cb"].dma(POOL, cb_s[:, :, gc * T:(gc + 1) * T], cbT[:], reads=[bf("cbT")],
                                        writes=[bf(f"cb_s{gc}")])

                if nAs > 1:
                    load_x(si, 1, slotA(1))
                pn1(0)
                gu_phase(F, 0, {})
                for t in range(nAs):
                    central = hl <= t < hl + ncn
                    gc = offC[si] + (t - hl)
                    dn_hooks = {}
                    if t + 1 < nAs:
                        spread(dn_hooks, 1, norm_parts(F, slotA(t + 1), 0, F.xn, "xnA"))
                    dn_phase(F, 0, dn_hooks)
                    pr = post_parts(F, slotA(t), 6, ghalf, 0, b_ghalf, F.ysb, "ysb")

                    def store_h(t=t, central=central, gc=gc):
                        if central:
                            ch_st["h"].dma(POOL, h_s[:, :, gc * T:(gc + 1) * T], xb[slotA(t)][:],
                                           reads=xbufs_of(slotA(t)), writes=[bf(f"h_s{gc}")])
                    ch_cs.dma(POOL, csb[:], cs_d[offA[si] + t], writes=[b_cs])
                    if t + 1 < nAs:
                        pr[0]()
                        hk = spread({}, 0, pr[1:])
                        hk.setdefault(4, []).append(store_h)
                        pm = norm_parts(F, slotA(t), 16, xnU, "xnU")
                        spread(hk, 6, pm)
                        if t + 2 < nAs:
                            hk.setdefault(11, []).append(lambda t=t: load_x(si, t + 2, slotA(t + 2)))
                        gu_phase(F, 0, hk)
                    else:
                        for th in pr:
                            th()
                        store_h()
                        pnm_and_prefetch(t)
                    proj_v_store(t)
                xsl_state["a"] = (s0 + nAs) % 2
                barrier()
                flush_block()

            with contextlib.ExitStack() as pes:
                def pb(name, shape, dt):
                    return pes.enter_context(nc.sbuf_tensor(f"{name}_B{si}", shape, dt))
                kring = [pb(f"kring{i}", [128, 4, T], BF16) for i in range(5)]
                vring = [pb(f"vring{i}", [128, 4, 584], BF16) for i in range(5)]
                qtiles = [pb(f"qt{i}", [128, 4, 2, T], BF16) for i in range(2)]
                for i in range(2):
                    POOL.emit(lambda e, i=i: e.memset(qtiles[i][:, 0:2, :, :], 0.0), writes=[bf(f"qtile{i}")])
                    POOL.emit(lambda e, i=i: e.memset(qtiles[i][:, 2:4, :, :], 0.0), writes=[bf(f"qtile{i}")])
                for i in range(5):
                    POOL.emit(lambda e, i=i: e.memset(vring[i][:, :, 520:584], 0.0), writes=[bf(f"ring{i}")])
                Eb = [pb(f"Eb{i}", [128, 2, T], BF16) for i in range(5)]
                Pb = [pb(f"Pb{i}", [128, 2, T], BF16) for i in range(5)]
                osb = [pb(f"osb{i}", [65, T], F32) for i in range(2)]
                attnT = pb("attnT", [64, 8, T], F32)
                ring_has = [None] * 5
                for tq in range(ncn):
                    ta = tq + hl
                    gc = offC[si] + tq
                    qi = tq % 2
                    qbuf = bf(f"qtile{qi}")

                    def load_q(tq_):
                        qi_ = tq_ % 2
                        gc_ = offC[si] + tq_
                        ch_ld[f"q{qi_}"].dma(SP, qtiles[qi_][0:64, :, 0, :], q_s[0:64, :, gc_ * T:(gc_ + 1) * T],
                                             reads=[bf(f"q_s{gc_}")], writes=[bf(f"qtile{qi_}")],
                                             more=[(qtiles[qi_][64:128, :, 1, :],
                                                    q_s[64:128, :, gc_ * T:(gc_ + 1) * T])])
                    if tq == 0:
                        load_q(0)
                    tiles = [a for a in range(ta - 2, ta + 3) if 0 <= a < nA[si]]
                    for a in tiles:
                        r = a % 5
                        if ring_has[r] != (si, a):
                            ring_has[r] = (si, a)
                            ga = offA[si] + a
                            ch_ring[r].dma(SP, kring[r][:], k_s[:, :, ga * T:(ga + 1) * T],
                                           reads=[bf(f"k_s{ga}"), bf(f"v_s{ga}")], writes=[bf(f"ring{r}")],
                                           chain=False,
                                           more=[(vring[r][:, :, 0:520], v_s[:, ga * 4:(ga + 1) * 4, :])])
                    if tq + 1 < ncn:
                        load_q(tq + 1)
                    blocks = []
                    for a in tiles:
                        for b in range(4):
                            delta = (a - ta) * T + b * 128
                            f0 = max(0, delta - 1024)
                            f1 = min(T, delta + 127 + 1024 + 1)
                            blocks.append((a, b, delta, f0, f1))
                    blocks.sort(key=lambda x: (x[2] != 0, x[2]))
                    assert blocks[0][3] == 0 and blocks[0][4] == T
                    qt = qtiles[qi]
                    full = [x for x in blocks if x[3] == 0 and x[4] == T]
                    part = [x for x in blocks if not (x[3] == 0 and x[4] == T)]
                    full.sort(key=lambda x: x[2])
                    units = []
                    i_ = 0
                    while i_ < len(full):
                        if i_ + 1 < len(full) and full[i_ + 1][2] == full[i_][2] + 128:
                            units.append([full[i_], full[i_ + 1]])
                            i_ += 2
                        else:
                            units.append([full[i_]])
                            i_ += 1
                    units += [[x] for x in part]
                    nmm = sum(len(u) for u in units)
                    assert nmm == len(blocks) and units[0][0][3] == 0 and units[0][0][4] == T
                    DEPTH = 3
                    obank_of = {}
                    pending = []

                    def emit_pv(h, unit, pi, first, last):
                        ob = obank_of[h]
                        for ui, (a, b, delta, f0, f1) in enumerate(unit):
                            r = a % 5
                            PE.emit_group([mm(banks[ob][:, f0:f1], vring[r][:, b, h * 65:h * 65 + 128],
                                              Pb[pi][:, ui, f0:f1], first and ui == 0,
                                              last and ui == len(unit) - 1)],
                                          reads=[bf(f"ring{r}"), bf(f"Pb{pi}")], writes=[bank_b[ob]])
                        if last:
                            oi = nxt("osb", 2)
                            DVE.emit(lambda e, oi=oi, ob=ob: e.tensor_copy(out=osb[oi][0:65, :], in_=banks[ob][0:65, :]),
                                     reads=[bank_b[ob]], writes=[bf(f"osb{oi}")])
                            DVE.emit(lambda e, oi=oi: e.reciprocal(out=osb[oi][64:65, :], in_=osb[oi][64:65, :]),
                                     reads=[bf(f"osb{oi}")], writes=[bf(f"osb{oi}")])
                            bcb = 2 * nxt("shalf", 3)
                            PE.emit_group([mm(banks[bcb][0:64, :], ones32[64:65, 0:64], osb[oi][64:65, :],
                                              True, True)],
                                          reads=[bf(f"osb{oi}"), b_ones], writes=[bank_b[bcb]])
                            DVE.emit(lambda e, oi=oi, bcb=bcb, h=h: e.tensor_tensor(
                                out=attnT[:, h, :], in0=osb[oi][0:64, :], in1=banks[bcb][0:64, :], op=ALU.mult),
                                reads=[bf(f"osb{oi}"), bank_b[bcb]], writes=[bf("attnT")])

                    for h in range(8):
                        j = h // 2
                        obank_of[h] = 6 + nxt("obank", 2)
                        for un, unit in enumerate(units):
                            half = nxt("shalf", 3)
                            sb0 = 2 * half
                            for ui, (a, b, delta, f0, f1) in enumerate(unit):
                                r = a % 5
                                PE.emit_group([mm(banks[sb0 + ui][:, f0:f1],
                                                  kring[r][:, j, b * 128:(b + 1) * 128],
                                                  qt[:, j, h % 2, f0:f1], True, True)],
                                              reads=[bf(f"ring{r}"), qbuf], writes=[bank_b[sb0 + ui]])
                            ei = nxt("E", 5)
                            pi = nxt("P", 5)
                            delta, f0, f1 = unit[0][2], unit[0][3], unit[0][4]
                            m0 = 1408 - delta
                            eng = POOL if (len(unit) == 1 and nxt("mk", 2) == 1) else DVE
                            if len(unit) == 2:
                                ACT.emit(act(Eb[ei][:].rearrange("p a t -> p (a t)"),
                                             bank03[:, sb0 * T:(sb0 + 2) * T], AF.Exp, scale=0.125),
                                         reads=[bank_b[sb0], bank_b[sb0 + 1]], writes=[bf(f"Eb{ei}")])
                                eng.emit(lambda e, pi=pi, ei=ei, m0=m0: e.tensor_tensor(
                                    out=Pb[pi][:], in0=Eb[ei][:], in1=maskT[:, :, m0:m0 + T], op=ALU.mult),
                                    reads=[bf(f"Eb{ei}"), b_mask], writes=[bf(f"Pb{pi}")])
                            else:
                                ACT.emit(act(Eb[ei][:, 0, f0:f1], banks[sb0][:, f0:f1], AF.Exp, scale=0.125),
                                         reads=[bank_b[sb0]], writes=[bf(f"Eb{ei}")])
                                eng.emit(lambda e, pi=pi, ei=ei, f0=f0, f1=f1, m0=m0: e.tensor_tensor(
                                    out=Pb[pi][:, 0, f0:f1], in0=Eb[ei][:, 0, f0:f1],
                                    in1=maskT[:, 0, m0 + f0:m0 + f1], op=ALU.mult),
                                    reads=[bf(f"Eb{ei}"), b_mask], writes=[bf(f"Pb{pi}")])
                            pending.append((h, unit, pi, un == 0, un == len(units) - 1))
                            if len(pending) > DEPTH:
                                emit_pv(*pending.pop(0))
                    while pending:
                        emit_pv(*pending.pop(0))
                    ch_st["a"].dma(POOL, a_s[:, :, gc * T:(gc + 1) * T], attnT[:], reads=[bf("attnT")],
                                   writes=[bf(f"a_s{gc}")])
                barrier()
                flush_block()

            with contextlib.ExitStack() as pes:
                F = FFNSet(pes, f"C{si}")

                def pc(name, shape, dt):
                    return pes.enter_context(nc.sbuf_tensor(f"{name}_C{si}", shape, dt))
                msb = pc("msb", [128, 8, T], F32)
                attnT = pc("attnT", [64, 8, T], F32)
                attn_n = pc("attn_n", [64, 8, T], BF16)
                cbT = pc("cbT", [128, 4, T], F32)
                ccvT = pc("ccvT", [128, 4, T + 2], F32)
                ctmp = [pc(f"ctmp{i}", [128, T], F32) for i in range(1)]
                convn = pc("convn", [128, 4, T], BF16)
                s0 = xsl_state["a"]
                last_stream = si + 1 >= len(specs)

                def slotC(t):
                    return (s0 + t) % 2

                def ld_h(t):
                    gc_ = offC[si] + t
                    xsl = slotC(t)
                    ch_x[xsl].dma(POOL, xb[xsl][:], h_s[:, :, gc_ * T:(gc_ + 1) * T],
                                  reads=[bf(f"h_s{gc_}")], writes=xbufs_of(xsl))

                def ld_front(t):
                    ta = t + hl
                    gc = offC[si] + t
                    ga = offA[si] + ta
                    ch_ld["a"].dma(POOL, attnT[:], a_s[:, :, gc * T:(gc + 1) * T], reads=[bf(f"a_s{gc}")],
                                   writes=[bf("attnT")])
                    ch_ld["cb"].dma(POOL, cbT[:], cb_s[:, :, gc * T:(gc + 1) * T], reads=[bf(f"cb_s{gc}")],
                                    writes=[bf("cbT")] + [bf(f"cbT{j}") for j in range(4)])
                    lo = 1 if ta == 0 else 0
                    hi = T + 1 if ta == nA[si] - 1 else T + 2
                    rd = [bf(f"ccv_s{ga}")]
                    if ta > 0:
                        rd.append(bf(f"ccv_s{ga - 1}"))
                    if ta < nA[si] - 1:
                        rd.append(bf(f"ccv_s{ga + 1}"))
                    if lo == 1:
                        POOL.emit(lambda e: e.memset(ccvT[:, :, 0:1], 0.0), writes=[bf("ccvT")])
                    if hi == T + 1:
                        POOL.emit(lambda e: e.memset(ccvT[:, :, T + 1:T + 2], 0.0), writes=[bf("ccvT")])
                    ch_ld["ccv"].dma(POOL, ccvT[:, :, lo:hi], ccv_s[:, :, ga * T - 1 + lo:ga * T - 1 + hi],
                                     reads=rd, writes=[bf("ccvT")])

                def fr_conv(j):
                    ci = nxt("ctmp", 1)
                    cb_ = bf(f"ctmp{ci}")
                    cw = lambda k: smalls[:, 60 + k * 4 + j:60 + k * 4 + j + 1]
                    DVE.emit(lambda e: e.tensor_scalar_mul(out=ctmp[ci][:], in0=ccvT[:, j, 0:T], scalar1=cw(0)),
                             reads=[bf("ccvT"), b_smalls], writes=[cb_])
                    DVE.emit(lambda e: e.scalar_tensor_tensor(
                        out=ctmp[ci][:], in0=ccvT[:, j, 1:T + 1], scalar=cw(1), in1=ctmp[ci][:],
                        op0=ALU.mult, op1=ALU.add), reads=[bf("ccvT"), b_smalls, cb_], writes=[cb_])
                    DVE.emit(lambda e: e.scalar_tensor_tensor(
                        out=ctmp[ci][:], in0=ccvT[:, j, 2:T + 2], scalar=cw(2), in1=ctmp[ci][:],
                        op0=ALU.mult, op1=ALU.add), reads=[bf("ccvT"), b_smalls, cb_], writes=[cb_])
                    POOL.emit(lambda e: e.tensor_tensor(out=cbT[:, j, :], in0=ctmp[ci][:], in1=cbT[:, j, :],
                                                        op=ALU.mult),
                              reads=[cb_, bf(f"cbT{j}"), bf("cbT")], writes=[bf(f"cbT{j}")])

                def fr_conv_norm():
                    rs_i = nxt("rs", 2)
                    rb = rms_rstd(F, [cbT[:, j, :] for j in range(4)], [bf(f"cbT{j}") for j in range(4)],
                                  onesm512, 7, 128, rs_i)
                    for j in range(4):
                        DVE.emit(lambda e, j=j: e.scalar_tensor_tensor(
                            out=convn[:, j, :], in0=cbT[:, j, :], scalar=smalls[:, 56 + j:57 + j],
                            in1=F.rstd[rs_i][:], op0=ALU.mult, op1=ALU.mult),
                            reads=[bf(f"cbT{j}"), rb, b_smalls], writes=[bf("convn")])

                def fr_attn_norm():
                    rs_i2 = nxt("rs", 2)
                    rb2 = rms_rstd(F, [attnT[:, h, :] for h in range(8)], [bf("attnT")] * 8, onesm512, 7, 64, rs_i2)
                    for h in range(8):
                        DVE.emit(lambda e, h=h: e.scalar_tensor_tensor(
                            out=attn_n[:, h, :], in0=attnT[:, h, :], scalar=smalls[0:64, 48 + h:49 + h],
                            in1=F.rstd[rs_i2][0:64, :], op0=ALU.mult, op1=ALU.mult),
                            reads=[bf("attnT"), rb2, b_smalls], writes=[bf("attn_n")])

                def wo(t):
                    pend = []
                    for u in range(4):
                        wt, wb = ws_get(("wo", u))
                        for o in range(2):
                            oc = 2 * u + o
                            yb = 4 + nxt("ybank", 2)
                            fns = [mm(banks[yb][:], wt[0:64, h * 256 + o * 128:h * 256 + (o + 1) * 128],
                                      attn_n[:, h, :], h == 0, False) for h in range(8)]
                            fns += [mm(banks[yb][:],
                                       wt[:, 2048 + kc * 256 + o * 128:2048 + kc * 256 + (o + 1) * 128],
                                       convn[:, kc, :], False, kc == 3) for kc in range(4)]
                            PE.emit_group(fns, reads=[wb, bf("attn_n"), bf("convn")], writes=[bank_b[yb]])
                            ACT.emit(act(msb[:, oc, :], banks[yb][:], AF.Copy), reads=[bank_b[yb]],
                                     writes=[bf(f"msb{oc}")])
                            sqi = nxt("dsq", 2)
                            ACT.emit(act(F.dsq[sqi][:], banks[yb][:], AF.Square), reads=[bank_b[yb]],
                                     writes=[bf(f"dsq{sqi}")])
                            if pend:
                                pend.pop(0)()
                            pend.append(lambda sqi=sqi, oc=oc: PE.emit_group(
                                [mm(banks[7][:], onesm1024[:], F.dsq[sqi][:], oc == 0, oc == 7)],
                                reads=[bf(f"dsq{sqi}"), b_ones], writes=[bank_b[7]]))
                    while pend:
                        pend.pop(0)()

                def prm(t):
                    post_residual(F, slotC(t), 7, smalls, 24, b_smalls, msb, "msb")

                def pn2(t):
                    norm_to(F, slotC(t), 32, F.xn, "xnA")

                def front_hooks(t, hk):
                    for j in range(4):
                        hk.setdefault(6 + 2 * j, []).append(lambda j=j: fr_conv(j))
                    hk.setdefault(14, []).append(fr_conv_norm)
                    hk.setdefault(16, []).append(fr_attn_norm)
                    hk.setdefault(18, []).append(lambda: ld_h(t))
                    return hk

                ld_front(0)
                ld_h(0)
                for j in range(4):
                    fr_conv(j)
                fr_conv_norm()
                fr_attn_norm()
                wo(0)
                if ncn > 1:
                    ld_front(1)
                prm(0)
                pn2(0)
                gu_phase(F, 1, front_hooks(1, {}) if ncn > 1 else {})
                for t in range(ncn):
                    gc = offC[si] + t
                    dn_phase_hooks = {}
                    if t + 1 < ncn:
                        wo(t + 1)
                        pp = post_parts(F, slotC(t + 1), 7, smalls, 24, b_smalls, msb, "msb")
                        dn_phase_hooks = {}
                        if t + 2 < ncn:
                            dn_phase_hooks[0] = [lambda t=t: ld_front(t + 2)]
                        dn_phase_hooks.setdefault(1, []).append(pp[0])
                        dn_phase_hooks.setdefault(2, []).extend(pp[1:3])
                        dn_phase_hooks.setdefault(3, []).extend(pp[3:5])
                        npn = norm_parts(F, slotC(t + 1), 32, F.xn, "xnA")
                        dn_phase_hooks.setdefault(5, []).append(npn[0])
                        dn_phase_hooks.setdefault(6, []).extend(npn[1:3])
                        dn_phase_hooks.setdefault(7, []).extend(npn[3:5])
                    dn_phase(F, 1, dn_phase_hooks)
                    pr = post_parts(F, slotC(t), 6, ghalf, 8, b_ghalf, F.ysb, "ysb")

                    def store_y(t=t, gc=gc):
                        ev = ch_st["y"].dma(POOL, yT3[:, :, gc * T:(gc + 1) * T], xb[slotC(t)][:],
                                            reads=xbufs_of(slotC(t)))
                        out_events.append(ev)
                    if t + 1 < ncn:
                        pr[0]()
                        hk = spread({}, 0, pr[1:])
                        hk.setdefault(4, []).append(store_y)
                        if t + 2 < ncn:
                            front_hooks(t + 2, hk)
                        elif not last_stream:
                            hk.setdefault(18, []).append(lambda: load_x(si + 1, 0, (s0 + ncn) % 2))
                        gu_phase(F, 1, hk)
                    else:
                        for th in pr:
                            th()
                        store_y()
                if ncn == 1 and not last_stream:
                    load_x(si + 1, 0, (s0 + ncn) % 2)
                xsl_state["a"] = (s0 + ncn) % 2
                if not last_stream:
                    barrier()
                    flush_block()
                else:
                    POOL.wait_ev(out_events[-1])
                    for q in (PE, ACT, DVE, SP):
                        q.wait_ev(out_events[-1])
                    flush_block()

        assert ws["next"] == len(plan), (ws["next"], len(plan))
        print(f"[kernel] blocks={blk_no[0]} counts: pe={PE.count} act={ACT.count} dve={DVE.count} "
              f"pool={POOL.count}", flush=True)
    return nc


def _prep_weights(inp):
    f32 = np.float32

    def gu(wg, wu):
        wg = wg.reshape(8, 128, NFC, 128)
        wu = wu.reshape(8, 128, NFC, 128)
        o = np.stack([wg, wu], 0)
        return np.ascontiguousarray(o.transpose(3, 2, 0, 1, 4)).reshape(NFC * 128, 2048)

    def dn(wd):
        w = wd.reshape(NFC, 128, 8, 128)
        return np.ascontiguousarray(w.transpose(2, 1, 0, 3)).reshape(8 * 128, DFF)

    w_in = inp["w_in"][0]
    cols = {}
    names = ["q", "k", "v", "cb", "cc", "cv"]
    for i, n in enumerate(names):
        cols[n] = w_in[:, i * 512:(i + 1) * 512]

    perm = np.array(list(range(0, 8)) + list(range(16, 40)) + list(range(8, 16)) + list(range(40, 64)))
    colperm = np.concatenate([h * 64 + perm for h in range(8)])
    qp, kp = cols["q"][:, colperm], cols["k"][:, colperm]
    chunks = []
    for j in range(4):
        chunks.append(qp[:, j * 128:(j + 1) * 128])
    for j in range(4):
        chunks.append(kp[:, j * 128:(j + 1) * 128])
    for j in range(4):
        chunks += [cols["cc"][:, j * 128:(j + 1) * 128], cols["cv"][:, j * 128:(j + 1) * 128]]
    for j in range(4):
        chunks += [cols["cb"][:, j * 128:(j + 1) * 128]]
    win = np.stack([c.reshape(8, 128, 128).transpose(1, 0, 2).reshape(128, 1024) for c in chunks], 0)
    win = np.ascontiguousarray(win).reshape(20 * 128, 1024)
    wv = np.ascontiguousarray(cols["v"].reshape(8, 128, 512).transpose(1, 0, 2)).reshape(128, 4096)
    w_out = inp["w_out"][0]
    woa = w_out[0:512].reshape(8, 64, 4, 256)
    woa = np.ascontiguousarray(woa.transpose(2, 1, 0, 3)).reshape(4 * 64, 2048)
    woc = w_out[512:1024].reshape(4, 128, 4, 256)
    woc = np.ascontiguousarray(woc.transpose(2, 1, 0, 3)).reshape(4 * 128, 1024)

    def g8(v):
        return v.reshape(8, 128).T

    smalls = np.zeros((128, 72), f32)
    for i, n in enumerate(["ffn1_pre_g", "ffn1_post_g", "mix_pre_g", "mix_post_g", "ffn2_pre_g", "ffn2_post_g"]):
        smalls[:, i * 8:(i + 1) * 8] = g8(inp[n][0])
    smalls[0:64, 48:56] = inp["attn_out_g"][0].reshape(8, 64).T
    smalls[:, 56:60] = inp["conv_out_g"][0].reshape(4, 128).T
    cw = inp["conv_w"][0]
    for k in range(3):
        smalls[:, 60 + 4 * k:64 + 4 * k] = cw[k].reshape(4, 128).T
    return {
        "gu1": gu(inp["ffn1_w_gate"][0], inp["ffn1_w_up"][0]).astype(f32),
        "gu2": gu(inp["ffn2_w_gate"][0], inp["ffn2_w_up"][0]).astype(f32),
        "dn1": dn(inp["ffn1_w_down"][0]).astype(f32),
        "dn2": dn(inp["ffn2_w_down"][0]).astype(f32),
        "win": win.astype(f32), "wv": wv.astype(f32), "woa": woa.astype(f32), "woc": woc.astype(f32),
        "smalls": smalls,
    }


def _mask_strip():
    j = np.arange(MW)[None, :]
    p = np.arange(128)[:, None]
    r = p - j + 1408
    ar = np.abs(r)
    c = (ar <= 64).astype(np.float32) + ((r % 4 == 0) & (ar <= 256)) + ((r % 16 == 0) & (ar <= 1024))
    c = c.astype(np.float32)
    c2 = np.zeros_like(c)
    c2[:, 128:] = c[:, :-128]
    return np.ascontiguousarray(np.concatenate([c, c2], 1))


def _rope_tables(pos):
    half = 8
    inv_freq = np.power(np.float32(ROPE_THETA), -np.arange(half, dtype=np.float32) * np.float32(2.0) / np.float32(16))
    ang = pos.astype(np.float32)[None, :] * inv_freq.astype(np.float32)[:, None]
    c, s = np.cos(ang).astype(np.float32), np.sin(ang).astype(np.float32)
    n = pos.shape[0]
    ct = np.ones((128, n), np.float32)
    st = np.zeros((128, n), np.float32)
    for hb in (0, 64):
        ct[hb:hb + 8] = c
        ct[hb + 32:hb + 40] = c
        st[hb:hb + 8] = -s
        st[hb + 32:hb + 40] = s
    return ct, st


def _run(inp, n_cores=8):
    xp = np.asarray(inp["x_prompt"], np.float32)
    xs = np.asarray(inp["x_sample"], np.float32)
    Bp, Sp, _ = xp.shape
    Bs, Ss, _ = xs.shape
    assert Bp == 2 * n_cores and Bs * 4 == n_cores
    tp = Sp // T
    chunk = Ss // 4
    tsn = chunk // T
    specs = [dict(nc=tp, hl=0, hr=0), dict(nc=tp, hl=0, hr=0), dict(nc=tsn, hl=2, hr=2)]
    nA = [s["nc"] + s["hl"] + s["hr"] for s in specs]
    NTA = sum(nA)
    wts = _prep_weights({k: np.asarray(v, np.float32) for k, v in inp.items() if k not in ("x_prompt", "x_sample")})
    maskT = _mask_strip()
    in_maps = []
    for c in range(n_cores):
        xcols, cs_l, valid_l = [], [], []
        srcs = [(xp[2 * c], 0, Sp), (xp[2 * c + 1], 0, Sp), (xs[c // 4], (c % 4) * chunk, Ss)]
        for s, (src, start, L) in zip(specs, srcs):
            t0 = start - s["hl"] * T
            t1 = start + (s["nc"] + s["hr"]) * T
            pos = np.arange(t0, t1)
            ok = (pos >= 0) & (pos < L)
            seg = np.zeros((t1 - t0, D), np.float32)
            seg[ok] = src[pos[ok]]
            xcols.append(seg.T)
            ct, st = _rope_tables(pos)
            cs_l.append((ct, st))
            valid_l.append(ok.astype(np.float32))
        xTc = np.ascontiguousarray(np.concatenate(xcols, 1))
        ct = np.concatenate([a for a, _ in cs_l], 1).reshape(128, NTA, T)
        st = np.concatenate([b for _, b in cs_l], 1).reshape(128, NTA, T)
        cs = np.ascontiguousarray(np.stack([ct, st], 2).transpose(1, 0, 2, 3)).reshape(NTA, 128, 2 * T)
        valid = np.concatenate(valid_l).reshape(NTA * 4, 128).T
        validx = np.ascontiguousarray(np.repeat(valid[:, :, None], 8, 2)).reshape(128, NTA * 4 * 8)
        m = {"xT": xTc, "cs": cs, "validx": validx.astype(np.float32), "maskT": maskT}
        m.update(wts)
        in_maps.append(m)
    nc = _build(specs)
    res = run_bass_kernel_spmd(nc, in_maps, core_ids=list(range(n_cores)))
    if DEBUG:
        _DBG["res"] = res.results
    yp = np.empty_like(xp)
    ys = np.empty_like(xs)
    for c in range(n_cores):
        y = res.results[c]["yT"]
        yp[2 * c] = y[:, 0:Sp].T
        yp[2 * c + 1] = y[:, Sp:2 * Sp].T
        ys[c // 4, (c % 4) * chunk:(c % 4 + 1) * chunk] = y[:, 2 * Sp:2 * Sp + chunk].T
    return yp, ys


def kernel(**inputs):
    return _run(inputs)
```

```python
import contextlib
import numpy as np
import concourse.bass as bass
import concourse.mybir as mybir
from concourse.bass_utils import run_bass_kernel_spmd

F32 = mybir.dt.float32
BF16 = mybir.dt.bfloat16
AF = mybir.ActivationFunctionType
ALU = mybir.AluOpType

D = 1024
DFF = 2816
NFC = DFF // 128
T = 512
EPS = 1e-6
MW = 2944
NSLOT = 6
SLOTW = 3072
ROPE_THETA = 500000.0


class Buf:
    __slots__ = ("name", "w", "rs")

    def __init__(self, name):
        self.name = name
        self.w = None
        self.rs = []


_CLOCK = {}


class Q:
    def __init__(self, name, sem, is_pe=False):
        self.name = name
        self.sem = sem
        self.count = 0
        self.ops = []
        self.waited = {}
        self.is_pe = is_pe

    def learn(self, ev):
        sem, val = ev
        k = id(sem)
        w = self.waited
        if w.get(k, 0) < val:
            w[k] = val
        snap = _CLOCK.get((k, val))
        if snap:
            for kk, vv in snap.items():
                if w.get(kk, 0) < vv:
                    w[kk] = vv

    def stamp(self, ev):
        _CLOCK[(id(ev[0]), ev[1])] = dict(self.waited)

    def wait_ev(self, ev):
        if ev is None:
            return
        sem, val = ev
        if sem is self.sem and self.is_pe:
            return
        k = id(sem)
        if self.waited.get(k, 0) >= val:
            return
        self.ops.append(("w", sem, val))
        self.learn(ev)

    def sync(self, reads, writes):
        for b in reads:
            self.wait_ev(b.w)
        for b in writes:
            if b.w is not None and b.w[0] is not self.sem:
                self.wait_ev(b.w)
            for r in b.rs:
                if r[0] is not self.sem:
                    self.wait_ev(r)

    def emit(self, fn, reads=(), writes=()):
        self.sync(reads, writes)
        self.count += 1
        ev = (self.sem, self.count)
        self.stamp(ev)
        self.ops.append(("i", fn, self.sem, 1))
        for b in reads:
            b.rs.append(ev)
        for b in writes:
            b.w = ev
            b.rs = []
        return ev

    def emit_group(self, fns, reads=(), writes=()):
        self.sync(reads, writes)
        for fn in fns[:-1]:
            self.ops.append(("n", fn))
        self.count += 1
        ev = (self.sem, self.count)
        self.stamp(ev)
        self.ops.append(("i", fns[-1], self.sem, 1))
        for b in reads:
            b.rs.append(ev)
        for b in writes:
            b.w = ev
            b.rs = []
        return ev


class Chan:
    def __init__(self, sem):
        self.sem = sem
        self.count = 0

    def dma(self, q, out, in_, reads=(), writes=(), chain=True, more=()):
        q.sync(reads, writes)
        if chain and self.count:
            q.wait_ev((self.sem, self.count))
        q.ops.append(("d", out, in_, self.sem))
        self.count += 16
        for (o2, i2) in more:
            q.ops.append(("d", o2, i2, self.sem))
            self.count += 16
        ev = (self.sem, self.count)
        q.stamp(ev)
        for b in reads:
            b.rs.append(ev)
        for b in writes:
            b.w = ev
            b.rs = []
        return ev


def _replay(q, eng):
    for op in q.ops:
        k = op[0]
        if k == "w":
            eng.wait_ge(op[1], op[2])
        elif k == "i":
            op[1](eng).then_inc(op[2], op[3])
        elif k == "n":
            op[1](eng)
        elif k == "d":
            eng.dma_start(out=op[1], in_=op[2]).then_inc(op[3], 16)


DEBUG = False
_DBG = {}


def _build(specs):
    _CLOCK.clear()
    nA = [s["nc"] + s["hl"] + s["hr"] for s in specs]
    nC = [s["nc"] for s in specs]
    offA = [sum(nA[:i]) for i in range(len(specs))]
    offC = [sum(nC[:i]) for i in range(len(specs))]
    NTA = sum(nA)
    NTC = sum(nC)
    NA = NTA * T
    NCT = NTC * T

    nc = bass.Bass("TRN2", target_bir_lowering=False)

    def din(name, shape, dt=F32):
        return nc.dram_tensor(name, shape, dt, kind="ExternalInput").ap()

    def dscr(name, shape, dt):
        if DEBUG and name.endswith("_s"):
            return nc.dram_tensor(name, shape, dt, kind="ExternalOutput").ap()
        return nc.dram_tensor(name, shape, dt, kind="Internal").ap()

    xT = din("xT", [D, NA])
    cs_d = din("cs", [NTA, 128, 2 * T])
    validx_d = din("validx", [128, NTA * 4 * 8])
    smalls_d = din("smalls", [128, 72])
    mask_d = din("maskT", [128, 2 * MW])
    gu_d = [din("gu1", [NFC * 128, 2048]), din("gu2", [NFC * 128, 2048])]
    dn_d = [din("dn1", [8 * 128, DFF]), din("dn2", [8 * 128, DFF])]
    win_d = din("win", [20 * 128, 1024])
    wv_d = din("wv", [128, 4096])
    woa_d = din("woa", [4 * 64, 2048])
    woc_d = din("woc", [4 * 128, 1024])
    yT = nc.dram_tensor("yT", [D, NCT], F32, kind="ExternalOutput").ap()

    gu_b = [dscr("gu1b", [NFC * 128, 2048], BF16), dscr("gu2b", [NFC * 128, 2048], BF16)]
    dn_b = [dscr("dn1b", [8 * 128, DFF], BF16), dscr("dn2b", [8 * 128, DFF], BF16)]
    win_b = dscr("winb", [20 * 128, 1024], BF16)
    wv_b = dscr("wvb", [128, 4096], BF16)
    woa_b = dscr("woab", [4 * 64, 2048], BF16)
    woc_b = dscr("wocb", [4 * 128, 1024], BF16)

    h_s = dscr("h_s", [128, 8, NCT], F32)
    q_s = dscr("q_s", [128, 4, NCT], BF16)
    k_s = dscr("k_s", [128, 4, NA], BF16)
    v_s = dscr("v_s", [128, NTA * 4, 520], BF16)
    cb_s = dscr("cb_s", [128, 4, NCT], F32)
    ccv_s = dscr("ccv_s", [128, 4, NA], F32)
    a_s = dscr("a_s", [64, 8, NCT], F32)

    xT3 = xT.rearrange("(k p) n -> p k n", p=128)
    yT3 = yT.rearrange("(k p) n -> p k n", p=128)

    es = contextlib.ExitStack()
    with es:
        def sb(name, shape, dt):
            return es.enter_context(nc.sbuf_tensor(name, shape, dt))

        def sem(name):
            return es.enter_context(nc.semaphore(name))

        PE = Q("pe", sem("s_pe"), is_pe=True)
        ACT = Q("act", sem("s_act"))
        DVE = Q("dve", sem("s_dve"))
        POOL = Q("pool", sem("s_pool"))
        SP = Q("sp", sem("s_sp"))

        ALLQ = (PE, ACT, DVE, POOL, SP)
        chans = []

        def chan(name):
            c = Chan(sem(name))
            chans.append(c)
            return c

        maskT = sb("maskT_sb", [128, 2, MW], BF16)
        smalls = sb("smalls_sb", [128, 72], F32)
        ghalf = sb("ghalf", [128, 16], F32)
        onesm1024 = sb("onesm1024", [128, 128], BF16)
        onesm512 = sb("onesm512", [128, 128], BF16)
        ones32 = sb("ones32", [128, 64], F32)
        wslot = [sb(f"wslot{i}", [128, SLOTW], BF16) for i in range(NSLOT)]
        xb = [sb(f"xb{i}", [128, 8, T], F32) for i in range(2)]
        bank03 = es.enter_context(nc.psum_tensor("bank05", [128, 6 * T], F32))
        banks = [bank03[:, i * T:(i + 1) * T] for i in range(6)]
        banks += [es.enter_context(nc.psum_tensor(f"bank{i}", [128, T], F32))[:] for i in range(6, 8)]

        B = {}

        def bf(name):
            if name not in B:
                B[name] = Buf(name)
            return B[name]

        bank_b = [bf(f"bank{i}") for i in range(8)]

        ch_const = chan("c_const")
        ch_cast = [chan(f"c_cast{i}") for i in range(5)]
        ch_slot = [chan(f"c_slot{i}") for i in range(NSLOT)]
        ch_x = [chan(f"c_x{i}") for i in range(2)]
        ch_cs = chan("c_cs")
        ch_st = {n: chan("c_st_" + n) for n in ["h", "q", "k", "v", "cb", "ccv", "a", "y"]}
        ch_ld = {n: chan("c_ld_" + n) for n in ["q0", "q1", "a", "cb", "ccv"]}
        ch_ring = [chan(f"c_ring{i}") for i in range(5)]

        blk_no = [0]

        def flush_block():
            with nc.Block() as block:
                @block.tensor
                def _(e):
                    _replay(PE, e)

                @block.scalar
                def _(e):
                    _replay(ACT, e)

                @block.vector
                def _(e):
                    _replay(DVE, e)

                @block.gpsimd
                def _(e):
                    _replay(POOL, e)

                @block.sync
                def _(e):
                    _replay(SP, e)
            for q in ALLQ:
                q.ops = []
            blk_no[0] += 1

        def barrier():
            evs = [(q.sem, q.count) for q in ALLQ if q.count] + [(c.sem, c.count) for c in chans if c.count]
            for q in ALLQ:
                for ev in evs:
                    q.wait_ev(ev)
            for b in B.values():
                b.w = None
                b.rs = []

        b_smalls, b_mask, b_valid, b_ones = bf("smalls"), bf("mask"), bf("valid"), bf("ones")
        ch_const.dma(POOL, smalls[:], smalls_d, writes=[b_smalls], chain=False,
                     more=[(maskT[:].rearrange("p a m -> p (a m)"), mask_d)])
        b_mask.w = b_smalls.w
        POOL.emit(lambda e: e.memset(onesm1024[:], 1.0 / 1024.0), writes=[b_ones])
        POOL.emit(lambda e: e.memset(onesm512[:], 1.0 / 512.0), writes=[b_ones])
        POOL.emit(lambda e: e.memset(ones32[:], 1.0), writes=[b_ones])
        b_ghalf = bf("ghalf")
        DVE.emit(lambda e: e.tensor_scalar_mul(out=ghalf[:, 0:8], in0=smalls[:, 8:16], scalar1=0.5),
                 reads=[b_smalls], writes=[b_ghalf])
        DVE.emit(lambda e: e.tensor_scalar_mul(out=ghalf[:, 8:16], in0=smalls[:, 40:48], scalar1=0.5),
                 reads=[b_smalls], writes=[b_ghalf])

        def cast(grp, dst, src, rows, step):
            for r0 in range(0, rows, step):
                r1 = min(rows, r0 + step)
                ch_cast[grp].dma(POOL, dst[r0:r1, :], src[r0:r1, :], chain=False)

        plan = []
        for si, s in enumerate(specs):
            n_ = nA[si]
            cen = [s["hl"] <= ta < s["hl"] + s["nc"] for ta in range(n_)]
            plan += [("gu", 0, fc) for fc in range(NFC)]
            for ta in range(n_):
                plan += [("dn", 0, dc) for dc in range(8)]
                if ta + 1 < n_:
                    plan += [("gu", 0, fc) for fc in range(NFC)]
                for u in range(10):
                    if not cen[ta] and (u < 2 or u >= 8):
                        continue
                    plan.append(("win", u))
                plan.append(("wv", 0))
                plan.append(("wv", 1))
            ncn_ = nC[si]
            plan += [("wo", u) for u in range(4)]
            plan += [("gu", 1, fc) for fc in range(NFC)]
            for tc in range(ncn_):
                if tc + 1 < ncn_:
                    plan += [("wo", u) for u in range(4)]
                plan += [("dn", 1, dc) for dc in range(8)]
                if tc + 1 < ncn_:
                    plan += [("gu", 1, fc) for fc in range(NFC)]

        wgrp_of = {"gu0": 0, "dn0": 1, "win": 2, "wv": 2, "wo": 2, "gu1": 3, "dn1": 4}
        slot_b = [bf(f"wslot{i}") for i in range(NSLOT)]
        ws = {"issued": 0, "next": 0, "grp_waited": set()}

        def ws_issue(i):
            unit = plan[i]
            sl = i % NSLOT
            t = wslot[sl]
            kind = unit[0]
            g = wgrp_of[kind + (str(unit[1]) if kind in ("gu", "dn") else "")]
            if g not in ws["grp_waited"]:
                ws["grp_waited"].add(g)
                SP.wait_ev((ch_cast[g].sem, ch_cast[g].count))
            ch = ch_slot[sl]
            wr = [slot_b[sl]]
            if kind == "gu":
                _, f, fc = unit
                ch.dma(SP, t[:, 0:2048], gu_b[f][fc * 128:(fc + 1) * 128, :], writes=wr, chain=False)
            elif kind == "dn":
                _, f, dc = unit
                ch.dma(SP, t[:, 0:DFF], dn_b[f][dc * 128:(dc + 1) * 128, :], writes=wr, chain=False)
            elif kind == "win":
                u = unit[1]
                oc = 2 * u
                ch.dma(SP, t[:, 0:1024], win_b[oc * 128:(oc + 1) * 128, :], writes=wr, chain=False,
                       more=[(t[:, 1024:2048], win_b[(oc + 1) * 128:(oc + 2) * 128, :])])
            elif kind == "wv":
                hv = unit[1]
                ch.dma(SP, t[:, 0:2048], wv_b[:, hv * 2048:(hv + 1) * 2048], writes=wr, chain=False)
            elif kind == "wo":
                u = unit[1]
                ch.dma(SP, t[0:64, 0:2048], woa_b[u * 64:(u + 1) * 64, :], writes=wr, chain=False,
                       more=[(t[:, 2048:3072], woc_b[u * 128:(u + 1) * 128, :])])

        def ws_get(expect):
            i = ws["next"]
            assert plan[i] == expect, (plan[i], expect)
            while ws["issued"] < min(len(plan), i + NSLOT - 1):
                ws_issue(ws["issued"])
                ws["issued"] += 1
            ws["next"] += 1
            sl = i % NSLOT
            return wslot[sl], slot_b[sl]

        rot = {}

        def nxt(key, n):
            v = rot.get(key, 0)
            rot[key] = v + 1
            return v % n

        def mm(out, lhsT, rhs, start, stop):
            return lambda e: e.matmul(out, lhsT=lhsT, rhs=rhs, start=start, stop=stop)

        def act(out, in_, func, scale=1.0, bias=0.0):
            return lambda e: e.activation(out=out, in_=in_, func=func, bias=bias, scale=scale)

        def xbufs_of(xsl):
            return [bf(f"xb{xsl}_{kc}") for kc in range(8)]

        class FFNSet:
            def __init__(self, pes, tag):
                def a(name, shape, dt):
                    return pes.enter_context(nc.sbuf_tensor(f"{name}_{tag}", shape, dt))
                self.xn = a("xn", [128, 8, T], BF16)
                self.hid = a("hid", [128, NFC, T], BF16)
                self.ysb = a("ysb", [128, 8, T], F32)
                self.sqb = [a(f"sqb{i}", [128, T], BF16) for i in range(3)]
                self.dsq = [a(f"dsq{i}", [128, T], BF16) for i in range(2)]
                self.sgb = [a(f"sgb{i}", [128, T], F32) for i in range(2)]
                self.rstd = [a(f"rstd{i}", [128, T], F32) for i in range(2)]
                self.t1b = [a(f"t1b{i}", [128, T], F32) for i in range(2)]

        def rms_rstd(F, chunk_aps, chunk_bufs, ones_ap, bank_i, npart, rs_i):
            n = len(chunk_aps)
            pend = []
            for i, (ap, b) in enumerate(zip(chunk_aps, chunk_bufs)):
                si_ = nxt("sq", 3)
                k = ap.shape[0]
                sq_ap = F.sqb[si_][0:k, :]
                ACT.emit(act(sq_ap, ap, AF.Square), reads=[b], writes=[bf(f"sqb{si_}")])
                if len(pend) >= 2:
                    pend.pop(0)()
                pend.append(lambda i=i, k=k, sq_ap=sq_ap, si_=si_: PE.emit_group(
                    [mm(banks[bank_i][0:npart, :], ones_ap[0:k, 0:npart], sq_ap, i == 0, i == n - 1)],
                    reads=[bf(f"sqb{si_}"), b_ones], writes=[bank_b[bank_i]]))
            while pend:
                pend.pop(0)()
            rb = bf(f"rstd{rs_i}")
            ACT.emit(act(F.rstd[rs_i][0:npart, :], banks[bank_i][0:npart, :], AF.Sqrt, bias=EPS),
                     reads=[bank_b[bank_i]], writes=[rb])
            DVE.emit(lambda e: e.reciprocal(out=F.rstd[rs_i][0:npart, :], in_=F.rstd[rs_i][0:npart, :]),
                     reads=[rb], writes=[rb])
            return rb

        def norm_parts(F, xsl, gcol0, xn_t, xn_name):
            xbufs = xbufs_of(xsl)
            st = {}

            def head():
                st["rs"] = nxt("rs", 2)
                st["rb"] = rms_rstd(F, [xb[xsl][:, kc, :] for kc in range(8)], xbufs, onesm1024, 7, 128, st["rs"])

            def part(k0):
                rs_i, rb = st["rs"], st["rb"]
                for kc in (k0, k0 + 1):
                    DVE.emit(lambda e, kc=kc: e.scalar_tensor_tensor(
                        out=xn_t[:, kc, :], in0=xb[xsl][:, kc, :], scalar=smalls[:, gcol0 + kc:gcol0 + kc + 1],
                        in1=F.rstd[rs_i][:], op0=ALU.mult, op1=ALU.mult),
                        reads=[xbufs[kc], rb, b_smalls], writes=[bf(f"{xn_name}{kc}")])
            return [head] + [lambda k0=k0: part(k0) for k0 in (0, 2, 4, 6)]

        def norm_to(F, xsl, gcol0, xn_t, xn_name):
            for th in norm_parts(F, xsl, gcol0, xn_t, xn_name):
                th()

        def post_parts(F, xsl, bank_i, gt, g0, gbuf, ysrc, yname):
            st = {}

            def head():
                rs_i = nxt("rs", 2)
                rb = bf(f"rstd{rs_i}")
                st["rs"], st["rb"] = rs_i, rb
                ACT.emit(act(F.rstd[rs_i][:], banks[bank_i][:], AF.Sqrt, bias=EPS), reads=[bank_b[bank_i]],
                         writes=[rb])
                DVE.emit(lambda e: e.reciprocal(out=F.rstd[rs_i][:], in_=F.rstd[rs_i][:]), reads=[rb], writes=[rb])

            def part(d0):
                rs_i, rb = st["rs"], st["rb"]
                for dc in (d0, d0 + 1):
                    ti = nxt("t1", 2)
                    DVE.emit(lambda e, dc=dc, ti=ti: e.scalar_tensor_tensor(
                        out=F.t1b[ti][:], in0=ysrc[:, dc, :], scalar=gt[:, g0 + dc:g0 + dc + 1],
                        in1=F.rstd[rs_i][:], op0=ALU.mult, op1=ALU.mult),
                        reads=[bf(f"{yname}{dc}"), rb, gbuf], writes=[bf(f"t1b{ti}")])
                    eng = POOL if dc % 2 == 0 else DVE
                    eng.emit(lambda e, dc=dc, ti=ti: e.tensor_tensor(
                        out=xb[xsl][:, dc, :], in0=xb[xsl][:, dc, :], in1=F.t1b[ti][:], op=ALU.add),
                        reads=[bf(f"t1b{ti}"), bf(f"xb{xsl}_{dc}")], writes=[bf(f"xb{xsl}_{dc}")])
            return [head] + [lambda d0=d0: part(d0) for d0 in (0, 2, 4, 6)]

        def post_residual(F, xsl, bank_i, gt, g0, gbuf, ysrc, yname):
            for th in post_parts(F, xsl, bank_i, gt, g0, gbuf, ysrc, yname):
                th()

        def spread(hooks, start, thunks, step=1):
            for i, th in enumerate(thunks):
                hooks.setdefault(start + i * step, []).append(th)
            return hooks

        def run_hooks(hooks, i):
            for th in hooks.get(i, ()):
                th()

        def gu_phase(F, f, hooks):
            xnb = [bf(f"xnA{kc}") for kc in range(8)]
            for fc in range(NFC):
                wt, wb = ws_get(("gu", f, fc))
                gi = nxt("gbank", 2)
                gb, ub = gi, 2 + gi
                PE.emit_group([mm(banks[gb][:], wt[:, kc * 128:(kc + 1) * 128], F.xn[:, kc, :], kc == 0, kc == 7)
                               for kc in range(8)], reads=[wb] + xnb, writes=[bank_b[gb]])
                PE.emit_group([mm(banks[ub][:], wt[:, 1024 + kc * 128:1024 + (kc + 1) * 128], F.xn[:, kc, :],
                                  kc == 0, kc == 7) for kc in range(8)], reads=[wb] + xnb, writes=[bank_b[ub]])
                sgi = nxt("sg", 2)
                ACT.emit(act(F.sgb[sgi][:], banks[gb][:], AF.Silu), reads=[bank_b[gb]], writes=[bf(f"sgb{sgi}")])
                DVE.emit(lambda e, fc=fc, sgi=sgi, ub=ub: e.tensor_tensor(
                    out=F.hid[:, fc, :], in0=F.sgb[sgi][:], in1=banks[ub][:], op=ALU.mult),
                    reads=[bf(f"sgb{sgi}"), bank_b[ub]], writes=[bf(f"hid{fc}")])
                run_hooks(hooks, fc)

        def dn_phase(F, f, hooks):
            hb = [bf(f"hid{fc}") for fc in range(NFC)]
            pend = []
            for dc in range(8):
                wt, wb = ws_get(("dn", f, dc))
                yb = 4 + nxt("ybank", 2)
                PE.emit_group([mm(banks[yb][:], wt[:, fc * 128:(fc + 1) * 128], F.hid[:, fc, :], fc == 0, fc == NFC - 1)
                               for fc in range(NFC)], reads=[wb] + hb, writes=[bank_b[yb]])
                ACT.emit(act(F.ysb[:, dc, :], banks[yb][:], AF.Copy), reads=[bank_b[yb]], writes=[bf(f"ysb{dc}")])
                si_ = nxt("dsq", 2)
                ACT.emit(act(F.dsq[si_][:], banks[yb][:], AF.Square), reads=[bank_b[yb]], writes=[bf(f"dsq{si_}")])
                if pend:
                    pend.pop(0)()
                pend.append(lambda si_=si_, dc=dc: PE.emit_group(
                    [mm(banks[6][:], onesm1024[:], F.dsq[si_][:], dc == 0, dc == 7)],
                    reads=[bf(f"dsq{si_}"), b_ones], writes=[bank_b[6]]))
                run_hooks(hooks, dc)
            while pend:
                pend.pop(0)()

        def load_x(si, ta, xsl):
            g = offA[si] + ta
            ch_x[xsl].dma(POOL, xb[xsl][:], xT3[:, :, g * T:(g + 1) * T], writes=xbufs_of(xsl))

        load_x(0, 0, 0)
        cast(0, gu_b[0], gu_d[0], NFC * 128, 512)
        cast(1, dn_b[0], dn_d[0], 8 * 128, 256)
        cast(2, win_b, win_d, 20 * 128, 1024)
        cast(2, wv_b, wv_d, 128, 128)
        cast(2, woa_b, woa_d, 256, 256)
        cast(2, woc_b, woc_d, 512, 512)
        cast(3, gu_b[1], gu_d[1], NFC * 128, 512)
        cast(4, dn_b[1], dn_d[1], 8 * 128, 256)

        xsl_state = {"a": 0}
        out_events = []

        for si, s in enumerate(specs):
            hl, ncn = s["hl"], s["nc"]
            with contextlib.ExitStack() as pes:
                F = FFNSet(pes, f"A{si}")

                def pa(name, shape, dt):
                    return pes.enter_context(nc.sbuf_tensor(f"{name}_A{si}", shape, dt))
                xnU = pa("xnU", [128, 8, T], BF16)
                validx = pa("validx", [128, nA[si] * 4 * 8], F32)
                ch_const.dma(POOL, validx[:], validx_d[:, offA[si] * 32:(offA[si] + nA[si]) * 32],
                             writes=[b_valid])
                t2b = [pa(f"t2b{i}", [128, T], F32) for i in range(2)]
                zsw = [pa(f"zsw{i}", [128, T], F32) for i in range(2)]
                zf = [pa(f"zf{i}", [128, T], F32) for i in range(2)]
                for i in range(2):
                    POOL.emit(lambda e, i=i: e.memset(zsw[i][:], 0.0), writes=[bf(f"zsw{i}")])
                csb = pa("csb", [128, 2 * T], F32)
                qT = pa("qT", [128, 4, T], BF16)
                kT = pa("kT", [128, 4, T], BF16)
                vaug = pa("vaug", [128, 4, 520], BF16)
                cbT = pa("cbT", [128, 4, T], F32)
                ccvT = pa("ccvT", [128, 4, T], F32)
                nAs = nA[si]
                s0 = xsl_state["a"]
                b_cs = bf("csb")

                def slotA(t):
                    return (s0 + t) % 2

                def pn1(t):
                    norm_to(F, slotA(t), 0, F.xn, "xnA")

                def pnm_and_prefetch(t):
                    norm_to(F, slotA(t), 16, xnU, "xnU")
                    if t + 2 < nAs:
                        load_x(si, t + 2, slotA(t + 2))

                def proj_v_store(t):
                    central = hl <= t < hl + ncn
                    g = offA[si] + t
                    gc = offC[si] + (t - hl)
                    xnb = [bf(f"xnU{kc}") for kc in range(8)]
                    for u in range(10):
                        if not central and (u < 2 or u >= 8):
                            continue
                        wt, wb = ws_get(("win", u))
                        pi = nxt("zbank", 2)
                        bA, bB = pi, 2 + pi
                        PE.emit_group([mm(banks[bA][:], wt[:, kc * 128:(kc + 1) * 128], xnU[:, kc, :],
                                          kc == 0, kc == 7) for kc in range(8)],
                                      reads=[wb] + xnb, writes=[bank_b[bA]])
                        PE.emit_group([mm(banks[bB][:], wt[:, 1024 + kc * 128:1024 + (kc + 1) * 128], xnU[:, kc, :],
                                          kc == 0, kc == 7) for kc in range(8)],
                                      reads=[wb] + xnb, writes=[bank_b[bB]])
                        if u < 4:
                            dst, dname = (qT, "qT") if u < 2 else (kT, "kT")
                            for ci_, bX in enumerate((bA, bB)):
                                j = 2 * (u % 2) + ci_
                                zi = nxt("zsw", 2)
                                zb_ = bf(f"zsw{zi}")
                                zfb = bf(f"zf{zi}")
                                ACT.emit(act(zf[zi][:], banks[bX][:], AF.Copy), reads=[bank_b[bX]], writes=[zfb])
                                for (o0, i0_) in ((0, 32), (32, 0), (64, 96), (96, 64)):
                                    ACT.emit(act(zsw[zi][o0:o0 + 8, :], zf[zi][i0_:i0_ + 8, :], AF.Copy),
                                             reads=[zfb], writes=[zb_])
                                ti = nxt("t1", 2)
                                t2i = nxt("t2", 2)
                                DVE.emit(lambda e, ti=ti, zi=zi: e.tensor_tensor(
                                    out=F.t1b[ti][:], in0=zf[zi][:], in1=csb[:, 0:T], op=ALU.mult),
                                    reads=[zfb, b_cs], writes=[bf(f"t1b{ti}")])
                                DVE.emit(lambda e, t2i=t2i, zi=zi: e.tensor_tensor(
                                    out=t2b[t2i][:], in0=zsw[zi][:], in1=csb[:, T:2 * T], op=ALU.mult),
                                    reads=[zb_, b_cs], writes=[bf(f"t2b{t2i}")])
                                POOL.emit(lambda e, ti=ti, t2i=t2i, dst=dst, j=j: e.tensor_tensor(
                                    out=dst[:, j, :], in0=F.t1b[ti][:], in1=t2b[t2i][:], op=ALU.add),
                                    reads=[bf(f"t1b{ti}"), bf(f"t2b{t2i}")], writes=[bf(dname)])
                        elif u < 8:
                            j = u - 4
                            ti = nxt("t1", 2)
                            ACT.emit(act(F.t1b[ti][:], banks[bA][:], AF.Copy), reads=[bank_b[bA]],
                                     writes=[bf(f"t1b{ti}")])
                            DVE.emit(lambda e, ti=ti, bB=bB, j=j: e.tensor_tensor(
                                out=ccvT[:, j, :], in0=F.t1b[ti][:], in1=banks[bB][:], op=ALU.mult),
                                reads=[bf(f"t1b{ti}"), bank_b[bB]], writes=[bf("ccvT")])
                        else:
                            j0 = 2 * (u - 8)
                            ACT.emit(act(cbT[:, j0, :], banks[bA][:], AF.Copy), reads=[bank_b[bA]],
                                     writes=[bf("cbT")])
                            ACT.emit(act(cbT[:, j0 + 1, :], banks[bB][:], AF.Copy), reads=[bank_b[bB]],
                                     writes=[bf("cbT")])
                    wt0, wb0 = ws_get(("wv", 0))
                    wt1, wb1 = ws_get(("wv", 1))
                    for blk in range(4):
                        vb = 4 + nxt("ybank", 2)
                        fns = []
                        for kc in range(8):
                            wt = wt0 if kc < 4 else wt1
                            kl = kc % 4
                            fns.append(mm(banks[vb][:], xnU[:, kc, blk * 128:(blk + 1) * 128],
                                          wt[:, kl * 512:(kl + 1) * 512], kc == 0, kc == 7))
                        PE.emit_group(fns, reads=[wb0, wb1] + xnb, writes=[bank_b[vb]])
                        vview = vaug[:, blk, :].rearrange("p (h d) -> p h d", d=65)
                        ACT.emit(lambda e, vb=vb, vview=vview: e.activation(
                            out=vview[:, :, 0:64], in_=banks[vb][:].rearrange("p (h d) -> p h d", d=64),
                            func=AF.Copy), reads=[bank_b[vb]], writes=[bf("vaug")])
                        vx = validx[:, (t * 4 + blk) * 8:(t * 4 + blk + 1) * 8].rearrange("p (h o) -> p h o", o=1)
                        DVE.emit(lambda e, vview=vview, vx=vx: e.tensor_copy(out=vview[:, :, 64:65], in_=vx),
                                 reads=[b_valid], writes=[bf("vaug")])
                    ch_st["k"].dma(POOL, k_s[:, :, g * T:(g + 1) * T], kT[:], reads=[bf("kT")],
                                   writes=[bf(f"k_s{g}")])
                    ch_st["v"].dma(POOL, v_s[:, g * 4:(g + 1) * 4, :], vaug[:], reads=[bf("vaug")],
                                   writes=[bf(f"v_s{g}")])
                    ch_st["ccv"].dma(POOL, ccv_s[:, :, g * T:(g + 1) * T], ccvT[:], reads=[bf("ccvT")],
                                     writes=[bf(f"ccv_s{g}")])
                    if central:
                        ch_st["q"].dma(POOL, q_s[:, :, gc * T:(gc + 1) * T], qT[:], reads=[bf("qT")],
                                       writes=[bf(f"q_s{gc}")])
                        ch_st["cb"].dma(POOL, cb_s[:, :, gc * T:(gc + 1) * T], cbT[:], reads=[bf("cbT")],
                                        writes=[bf(f"cb_s{gc}")])

                if nAs > 1:
                    load_x(si, 1, slotA(1))
                pn1(0)
                gu_phase(F, 0, {})
                for t in range(nAs):
                    central = hl <= t < hl + ncn
                    gc = offC[si] + (t - hl)
                    dn_hooks = {}
                    if t + 1 < nAs:
                        spread(dn_hooks, 1, norm_parts(F, slotA(t + 1), 0, F.xn, "xnA"))
                    dn_phase(F, 0, dn_hooks)
                    pr = post_parts(F, slotA(t), 6, ghalf, 0, b_ghalf, F.ysb, "ysb")

                    def store_h(t=t, central=central, gc=gc):
                        if central:
                            ch_st["h"].dma(POOL, h_s[:, :, gc * T:(gc + 1) * T], xb[slotA(t)][:],
                                           reads=xbufs_of(slotA(t)), writes=[bf(f"h_s{gc}")])
                    ch_cs.dma(POOL, csb[:], cs_d[offA[si] + t], writes=[b_cs])
                    if t + 1 < nAs:
                        pr[0]()
                        hk = spread({}, 0, pr[1:])
                        hk.setdefault(4, []).append(store_h)
                        pm = norm_parts(F, slotA(t), 16, xnU, "xnU")
                        spread(hk, 6, pm)
                        if t + 2 < nAs:
                            hk.setdefault(11, []).append(lambda t=t: load_x(si, t + 2, slotA(t + 2)))
                        gu_phase(F, 0, hk)
                    else:
                        for th in pr:
                            th()
                        store_h()
                        pnm_and_prefetch(t)
                    proj_v_store(t)
                xsl_state["a"] = (s0 + nAs) % 2
                barrier()
                flush_block()

            with contextlib.ExitStack() as pes:
                def pb(name, shape, dt):
                    return pes.enter_context(nc.sbuf_tensor(f"{name}_B{si}", shape, dt))
                kring = [pb(f"kring{i}", [128, 4, T], BF16) for i in range(5)]
                vring = [pb(f"vring{i}", [128, 4, 584], BF16) for i in range(5)]
                qtiles = [pb(f"qt{i}", [128, 4, 2, T], BF16) for i in range(2)]
                for i in range(2):
                    POOL.emit(lambda e, i=i: e.memset(qtiles[i][:, 0:2, :, :], 0.0), writes=[bf(f"qtile{i}")])
                    POOL.emit(lambda e, i=i: e.memset(qtiles[i][:, 2:4, :, :], 0.0), writes=[bf(f"qtile{i}")])
                for i in range(5):
                    POOL.emit(lambda e, i=i: e.memset(vring[i][:, :, 520:584], 0.0), writes=[bf(f"ring{i}")])
                Eb = [pb(f"Eb{i}", [128, 2, T], BF16) for i in range(5)]
                Pb = [pb(f"Pb{i}", [128, 2, T], BF16) for i in range(5)]
                osb = [pb(f"osb{i}", [65, T], F32) for i in range(2)]
                attnT = pb("attnT", [64, 8, T], F32)
                ring_has = [None] * 5
                for tq in range(ncn):
                    ta = tq + hl
                    gc = offC[si] + tq
                    qi = tq % 2
                    qbuf = bf(f"qtile{qi}")
                    ch_ld[f"q{qi}"].dma(SP, qtiles[qi][0:64, :, 0, :], q_s[0:64, :, gc * T:(gc + 1) * T],
                                        reads=[bf(f"q_s{gc}")], writes=[qbuf],
                                        more=[(qtiles[qi][64:128, :, 1, :], q_s[64:128, :, gc * T:(gc + 1) * T])])
                    tiles = [a for a in range(ta - 2, ta + 3) if 0 <= a < nA[si]]
                    for a in tiles:
                        r = a % 5
                        if ring_has[r] != (si, a):
                            ring_has[r] = (si, a)
                            ga = offA[si] + a
                            ch_ring[r].dma(SP, kring[r][:], k_s[:, :, ga * T:(ga + 1) * T],
                                           reads=[bf(f"k_s{ga}"), bf(f"v_s{ga}")], writes=[bf(f"ring{r}")],
                                           chain=False,
                                           more=[(vring[r][:, :, 0:520], v_s[:, ga * 4:(ga + 1) * 4, :])])
                    blocks = []
                    for a in tiles:
                        for b in range(4):
                            delta = (a - ta) * T + b * 128
                            f0 = max(0, delta - 1024)
                            f1 = min(T, delta + 127 + 1024 + 1)
                            blocks.append((a, b, delta, f0, f1))
                    blocks.sort(key=lambda x: (x[2] != 0, x[2]))
                    assert blocks[0][3] == 0 and blocks[0][4] == T
                    qt = qtiles[qi]
                    full = [x for x in blocks if x[3] == 0 and x[4] == T]
                    part = [x for x in blocks if not (x[3] == 0 and x[4] == T)]
                    full.sort(key=lambda x: x[2])
                    units = []
                    i_ = 0
                    while i_ < len(full):
                        if i_ + 1 < len(full) and full[i_ + 1][2] == full[i_][2] + 128:
                            units.append([full[i_], full[i_ + 1]])
                            i_ += 2
                        else:
                            units.append([full[i_]])
                            i_ += 1
                    units += [[x] for x in part]
                    nmm = sum(len(u) for u in units)
                    assert nmm == len(blocks) and units[0][0][3] == 0 and units[0][0][4] == T
                    DEPTH = 3
                    obank_of = {}
                    pending = []

                    def emit_pv(h, unit, pi, first, last):
                        ob = obank_of[h]
                        for ui, (a, b, delta, f0, f1) in enumerate(unit):
                            r = a % 5
                            PE.emit_group([mm(banks[ob][:, f0:f1], vring[r][:, b, h * 65:h * 65 + 128],
                                              Pb[pi][:, ui, f0:f1], first and ui == 0,
                                              last and ui == len(unit) - 1)],
                                          reads=[bf(f"ring{r}"), bf(f"Pb{pi}")], writes=[bank_b[ob]])
                        if last:
                            oi = nxt("osb", 2)
                            ACT.emit(act(osb[oi][0:65, :], banks[ob][0:65, :], AF.Copy), reads=[bank_b[ob]],
                                     writes=[bf(f"osb{oi}")])
                            DVE.emit(lambda e, oi=oi: e.reciprocal(out=osb[oi][64:65, :], in_=osb[oi][64:65, :]),
                                     reads=[bf(f"osb{oi}")], writes=[bf(f"osb{oi}")])
                            bcb = 2 * nxt("shalf", 3)
                            PE.emit_group([mm(banks[bcb][0:64, :], ones32[64:65, 0:64], osb[oi][64:65, :],
                                              True, True)],
                                          reads=[bf(f"osb{oi}"), b_ones], writes=[bank_b[bcb]])
                            DVE.emit(lambda e, oi=oi, bcb=bcb, h=h: e.tensor_tensor(
                                out=attnT[:, h, :], in0=osb[oi][0:64, :], in1=banks[bcb][0:64, :], op=ALU.mult),
                                reads=[bf(f"osb{oi}"), bank_b[bcb]], writes=[bf("attnT")])

                    for h in range(8):
                        j = h // 2
                        obank_of[h] = 6 + nxt("obank", 2)
                        for un, unit in enumerate(units):
                            half = nxt("shalf", 3)
                            sb0 = 2 * half
                            for ui, (a, b, delta, f0, f1) in enumerate(unit):
                                r = a % 5
                                PE.emit_group([mm(banks[sb0 + ui][:, f0:f1],
                                                  kring[r][:, j, b * 128:(b + 1) * 128],
                                                  qt[:, j, h % 2, f0:f1], True, True)],
                                              reads=[bf(f"ring{r}"), qbuf], writes=[bank_b[sb0 + ui]])
                            ei = nxt("E", 5)
                            pi = nxt("P", 5)
                            delta, f0, f1 = unit[0][2], unit[0][3], unit[0][4]
                            m0 = 1408 - delta
                            eng = POOL if (len(unit) == 1 and nxt("mk", 2) == 1) else DVE
                            if len(unit) == 2:
                                ACT.emit(act(Eb[ei][:].rearrange("p a t -> p (a t)"),
                                             bank03[:, sb0 * T:(sb0 + 2) * T], AF.Exp, scale=0.125),
                                         reads=[bank_b[sb0], bank_b[sb0 + 1]], writes=[bf(f"Eb{ei}")])
                                eng.emit(lambda e, pi=pi, ei=ei, m0=m0: e.tensor_tensor(
                                    out=Pb[pi][:], in0=Eb[ei][:], in1=maskT[:, :, m0:m0 + T], op=ALU.mult),
                                    reads=[bf(f"Eb{ei}"), b_mask], writes=[bf(f"Pb{pi}")])
                            else:
                                ACT.emit(act(Eb[ei][:, 0, f0:f1], banks[sb0][:, f0:f1], AF.Exp, scale=0.125),
                                         reads=[bank_b[sb0]], writes=[bf(f"Eb{ei}")])
                                eng.emit(lambda e, pi=pi, ei=ei, f0=f0, f1=f1, m0=m0: e.tensor_tensor(
                                    out=Pb[pi][:, 0, f0:f1], in0=Eb[ei][:, 0, f0:f1],
                                    in1=maskT[:, 0, m0 + f0:m0 + f1], op=ALU.mult),
                                    reads=[bf(f"Eb{ei}"), b_mask], writes=[bf(f"Pb{pi}")])
                            pending.append((h, unit, pi, un == 0, un == len(units) - 1))
                            if len(pending) > DEPTH:
                                emit_pv(*pending.pop(0))
                    while pending:
                        emit_pv(*pending.pop(0))
                    ch_st["a"].dma(POOL, a_s[:, :, gc * T:(gc + 1) * T], attnT[:], reads=[bf("attnT")],
                                   writes=[bf(f"a_s{gc}")])
                barrier()
                flush_block()

            with contextlib.ExitStack() as pes:
                F = FFNSet(pes, f"C{si}")

                def pc(name, shape, dt):
                    return pes.enter_context(nc.sbuf_tensor(f"{name}_C{si}", shape, dt))
                msb = pc("msb", [128, 8, T], F32)
                attnT = pc("attnT", [64, 8, T], F32)
                attn_n = pc("attn_n", [64, 8, T], BF16)
                cbT = pc("cbT", [128, 4, T], F32)
                ccvT = pc("ccvT", [128, 4, T + 2], F32)
                ctmp = [pc(f"ctmp{i}", [128, T], F32) for i in range(1)]
                convn = pc("convn", [128, 4, T], BF16)
                s0 = xsl_state["a"]
                last_stream = si + 1 >= len(specs)

                def slotC(t):
                    return (s0 + t) % 2

                def ld_h(t):
                    gc_ = offC[si] + t
                    xsl = slotC(t)
                    ch_x[xsl].dma(POOL, xb[xsl][:], h_s[:, :, gc_ * T:(gc_ + 1) * T],
                                  reads=[bf(f"h_s{gc_}")], writes=xbufs_of(xsl))

                def ld_front(t):
                    ta = t + hl
                    gc = offC[si] + t
                    ga = offA[si] + ta
                    ch_ld["a"].dma(POOL, attnT[:], a_s[:, :, gc * T:(gc + 1) * T], reads=[bf(f"a_s{gc}")],
                                   writes=[bf("attnT")])
                    ch_ld["cb"].dma(POOL, cbT[:], cb_s[:, :, gc * T:(gc + 1) * T], reads=[bf(f"cb_s{gc}")],
                                    writes=[bf("cbT")] + [bf(f"cbT{j}") for j in range(4)])
                    lo = 1 if ta == 0 else 0
                    hi = T + 1 if ta == nA[si] - 1 else T + 2
                    rd = [bf(f"ccv_s{ga}")]
                    if ta > 0:
                        rd.append(bf(f"ccv_s{ga - 1}"))
                    if ta < nA[si] - 1:
                        rd.append(bf(f"ccv_s{ga + 1}"))
                    if lo == 1:
                        POOL.emit(lambda e: e.memset(ccvT[:, :, 0:1], 0.0), writes=[bf("ccvT")])
                    if hi == T + 1:
                        POOL.emit(lambda e: e.memset(ccvT[:, :, T + 1:T + 2], 0.0), writes=[bf("ccvT")])
                    ch_ld["ccv"].dma(POOL, ccvT[:, :, lo:hi], ccv_s[:, :, ga * T - 1 + lo:ga * T - 1 + hi],
                                     reads=rd, writes=[bf("ccvT")])

                def fr_conv(j):
                    ci = nxt("ctmp", 1)
                    cb_ = bf(f"ctmp{ci}")
                    cw = lambda k: smalls[:, 60 + k * 4 + j:60 + k * 4 + j + 1]
                    DVE.emit(lambda e: e.tensor_scalar_mul(out=ctmp[ci][:], in0=ccvT[:, j, 0:T], scalar1=cw(0)),
                             reads=[bf("ccvT"), b_smalls], writes=[cb_])
                    DVE.emit(lambda e: e.scalar_tensor_tensor(
                        out=ctmp[ci][:], in0=ccvT[:, j, 1:T + 1], scalar=cw(1), in1=ctmp[ci][:],
                        op0=ALU.mult, op1=ALU.add), reads=[bf("ccvT"), b_smalls, cb_], writes=[cb_])
                    DVE.emit(lambda e: e.scalar_tensor_tensor(
                        out=ctmp[ci][:], in0=ccvT[:, j, 2:T + 2], scalar=cw(2), in1=ctmp[ci][:],
                        op0=ALU.mult, op1=ALU.add), reads=[bf("ccvT"), b_smalls, cb_], writes=[cb_])
                    POOL.emit(lambda e: e.tensor_tensor(out=cbT[:, j, :], in0=ctmp[ci][:], in1=cbT[:, j, :],
                                                        op=ALU.mult),
                              reads=[cb_, bf(f"cbT{j}"), bf("cbT")], writes=[bf(f"cbT{j}")])

                def fr_conv_norm():
                    rs_i = nxt("rs", 2)
                    rb = rms_rstd(F, [cbT[:, j, :] for j in range(4)], [bf(f"cbT{j}") for j in range(4)],
                                  onesm512, 7, 128, rs_i)
                    for j in range(4):
                        DVE.emit(lambda e, j=j: e.scalar_tensor_tensor(
                            out=convn[:, j, :], in0=cbT[:, j, :], scalar=smalls[:, 56 + j:57 + j],
                            in1=F.rstd[rs_i][:], op0=ALU.mult, op1=ALU.mult),
                            reads=[bf(f"cbT{j}"), rb, b_smalls], writes=[bf("convn")])

                def fr_attn_norm():
                    rs_i2 = nxt("rs", 2)
                    rb2 = rms_rstd(F, [attnT[:, h, :] for h in range(8)], [bf("attnT")] * 8, onesm512, 7, 64, rs_i2)
                    for h in range(8):
                        DVE.emit(lambda e, h=h: e.scalar_tensor_tensor(
                            out=attn_n[:, h, :], in0=attnT[:, h, :], scalar=smalls[0:64, 48 + h:49 + h],
                            in1=F.rstd[rs_i2][0:64, :], op0=ALU.mult, op1=ALU.mult),
                            reads=[bf("attnT"), rb2, b_smalls], writes=[bf("attn_n")])

                def wo(t):
                    pend = []
                    for u in range(4):
                        wt, wb = ws_get(("wo", u))
                        for o in range(2):
                            oc = 2 * u + o
                            yb = 4 + nxt("ybank", 2)
                            fns = [mm(banks[yb][:], wt[0:64, h * 256 + o * 128:h * 256 + (o + 1) * 128],
                                      attn_n[:, h, :], h == 0, False) for h in range(8)]
                            fns += [mm(banks[yb][:],
                                       wt[:, 2048 + kc * 256 + o * 128:2048 + kc * 256 + (o + 1) * 128],
                                       convn[:, kc, :], False, kc == 3) for kc in range(4)]
                            PE.emit_group(fns, reads=[wb, bf("attn_n"), bf("convn")], writes=[bank_b[yb]])
                            ACT.emit(act(msb[:, oc, :], banks[yb][:], AF.Copy), reads=[bank_b[yb]],
                                     writes=[bf(f"msb{oc}")])
                            sqi = nxt("dsq", 2)
                            ACT.emit(act(F.dsq[sqi][:], banks[yb][:], AF.Square), reads=[bank_b[yb]],
                                     writes=[bf(f"dsq{sqi}")])
                            if pend:
                                pend.pop(0)()
                            pend.append(lambda sqi=sqi, oc=oc: PE.emit_group(
                                [mm(banks[7][:], onesm1024[:], F.dsq[sqi][:], oc == 0, oc == 7)],
                                reads=[bf(f"dsq{sqi}"), b_ones], writes=[bank_b[7]]))
                    while pend:
                        pend.pop(0)()

                def prm(t):
                    post_residual(F, slotC(t), 7, smalls, 24, b_smalls, msb, "msb")

                def pn2(t):
                    norm_to(F, slotC(t), 32, F.xn, "xnA")

                def front_hooks(t, hk):
                    for j in range(4):
                        hk.setdefault(6 + 2 * j, []).append(lambda j=j: fr_conv(j))
                    hk.setdefault(14, []).append(fr_conv_norm)
                    hk.setdefault(16, []).append(fr_attn_norm)
                    hk.setdefault(18, []).append(lambda: ld_h(t))
                    return hk

                ld_front(0)
                ld_h(0)
                for j in range(4):
                    fr_conv(j)
                fr_conv_norm()
                fr_attn_norm()
                wo(0)
                if ncn > 1:
                    ld_front(1)
                prm(0)
                pn2(0)
                gu_phase(F, 1, front_hooks(1, {}) if ncn > 1 else {})
                for t in range(ncn):
                    gc = offC[si] + t
                    dn_phase_hooks = {}
                    if t + 1 < ncn:
                        wo(t + 1)
                        pp = post_parts(F, slotC(t + 1), 7, smalls, 24, b_smalls, msb, "msb")
                        dn_phase_hooks = {}
                        if t + 2 < ncn:
                            dn_phase_hooks[0] = [lambda t=t: ld_front(t + 2)]
                        dn_phase_hooks.setdefault(1, []).append(pp[0])
                        dn_phase_hooks.setdefault(2, []).extend(pp[1:3])
                        dn_phase_hooks.setdefault(3, []).extend(pp[3:5])
                        npn = norm_parts(F, slotC(t + 1), 32, F.xn, "xnA")
                        dn_phase_hooks.setdefault(5, []).append(npn[0])
                        dn_phase_hooks.setdefault(6, []).extend(npn[1:3])
                        dn_phase_hooks.setdefault(7, []).extend(npn[3:5])
                    dn_phase(F, 1, dn_phase_hooks)
                    pr = post_parts(F, slotC(t), 6, ghalf, 8, b_ghalf, F.ysb, "ysb")

                    def store_y(t=t, gc=gc):
                        ev = ch_st["y"].dma(POOL, yT3[:, :, gc * T:(gc + 1) * T], xb[slotC(t)][:],
                                            reads=xbufs_of(slotC(t)))
                        out_events.append(ev)
                    if t + 1 < ncn:
                        pr[0]()
                        hk = spread({}, 0, pr[1:])
                        hk.setdefault(4, []).append(store_y)
                        if t + 2 < ncn:
                            front_hooks(t + 2, hk)
                        elif not last_stream:
                            hk.setdefault(18, []).append(lambda: load_x(si + 1, 0, (s0 + ncn) % 2))
                        gu_phase(F, 1, hk)
                    else:
                        for th in pr:
                            th()
                        store_y()
                if ncn == 1 and not last_stream:
                    load_x(si + 1, 0, (s0 + ncn) % 2)
                xsl_state["a"] = (s0 + ncn) % 2
                if not last_stream:
                    barrier()
                    flush_block()
                else:
                    POOL.wait_ev(out_events[-1])
                    for q in (PE, ACT, DVE, SP):
                        q.wait_ev(out_events[-1])
                    flush_block()

        assert ws["next"] == len(plan), (ws["next"], len(plan))
        print(f"[kernel] blocks={blk_no[0]} counts: pe={PE.count} act={ACT.count} dve={DVE.count} "
              f"pool={POOL.count}", flush=True)
    return nc


def _prep_weights(inp):
    f32 = np.float32

    def gu(wg, wu):
        wg = wg.reshape(8, 128, NFC, 128)
        wu = wu.reshape(8, 128, NFC, 128)
        o = np.stack([wg, wu], 0)
        return np.ascontiguousarray(o.transpose(3, 2, 0, 1, 4)).reshape(NFC * 128, 2048)

    def dn(wd):
        w = wd.reshape(NFC, 128, 8, 128)
        return np.ascontiguousarray(w.transpose(2, 1, 0, 3)).reshape(8 * 128, DFF)

    w_in = inp["w_in"][0]
    cols = {}
    names = ["q", "k", "v", "cb", "cc", "cv"]
    for i, n in enumerate(names):
        cols[n] = w_in[:, i * 512:(i + 1) * 512]

    perm = np.array(list(range(0, 8)) + list(range(16, 40)) + list(range(8, 16)) + list(range(40, 64)))
    colperm = np.concatenate([h * 64 + perm for h in range(8)])
    qp, kp = cols["q"][:, colperm], cols["k"][:, colperm]
    chunks = []
    for j in range(4):
        chunks.append(qp[:, j * 128:(j + 1) * 128])
    for j in range(4):
        chunks.append(kp[:, j * 128:(j + 1) * 128])
    for j in range(4):
        chunks += [cols["cc"][:, j * 128:(j + 1) * 128], cols["cv"][:, j * 128:(j + 1) * 128]]
    for j in range(4):
        chunks += [cols["cb"][:, j * 128:(j + 1) * 128]]
    win = np.stack([c.reshape(8, 128, 128).transpose(1, 0, 2).reshape(128, 1024) for c in chunks], 0)
    win = np.ascontiguousarray(win).reshape(20 * 128, 1024)
    wv = np.ascontiguousarray(cols["v"].reshape(8, 128, 512).transpose(1, 0, 2)).reshape(128, 4096)
    w_out = inp["w_out"][0]
    woa = w_out[0:512].reshape(8, 64, 4, 256)
    woa = np.ascontiguousarray(woa.transpose(2, 1, 0, 3)).reshape(4 * 64, 2048)
    woc = w_out[512:1024].reshape(4, 128, 4, 256)
    woc = np.ascontiguousarray(woc.transpose(2, 1, 0, 3)).reshape(4 * 128, 1024)

    def g8(v):
        return v.reshape(8, 128).T

    smalls = np.zeros((128, 72), f32)
    for i, n in enumerate(["ffn1_pre_g", "ffn1_post_g", "mix_pre_g", "mix_post_g", "ffn2_pre_g", "ffn2_post_g"]):
        smalls[:, i * 8:(i + 1) * 8] = g8(inp[n][0])
    smalls[0:64, 48:56] = inp["attn_out_g"][0].reshape(8, 64).T
    smalls[:, 56:60] = inp["conv_out_g"][0].reshape(4, 128).T
    cw = inp["conv_w"][0]
    for k in range(3):
        smalls[:, 60 + 4 * k:64 + 4 * k] = cw[k].reshape(4, 128).T
    return {
        "gu1": gu(inp["ffn1_w_gate"][0], inp["ffn1_w_up"][0]).astype(f32),
        "gu2": gu(inp["ffn2_w_gate"][0], inp["ffn2_w_up"][0]).astype(f32),
        "dn1": dn(inp["ffn1_w_down"][0]).astype(f32),
        "dn2": dn(inp["ffn2_w_down"][0]).astype(f32),
        "win": win.astype(f32), "wv": wv.astype(f32), "woa": woa.astype(f32), "woc": woc.astype(f32),
        "smalls": smalls,
    }


def _mask_strip():
    j = np.arange(MW)[None, :]
    p = np.arange(128)[:, None]
    r = p - j + 1408
    ar = np.abs(r)
    c = (ar <= 64).astype(np.float32) + ((r % 4 == 0) & (ar <= 256)) + ((r % 16 == 0) & (ar <= 1024))
    c = c.astype(np.float32)
    c2 = np.zeros_like(c)
    c2[:, 128:] = c[:, :-128]
    return np.ascontiguousarray(np.concatenate([c, c2], 1))


def _rope_tables(pos):
    half = 8
    inv_freq = np.power(np.float32(ROPE_THETA), -np.arange(half, dtype=np.float32) * np.float32(2.0) / np.float32(16))
    ang = pos.astype(np.float32)[None, :] * inv_freq.astype(np.float32)[:, None]
    c, s = np.cos(ang).astype(np.float32), np.sin(ang).astype(np.float32)
    n = pos.shape[0]
    ct = np.ones((128, n), np.float32)
    st = np.zeros((128, n), np.float32)
    for hb in (0, 64):
        ct[hb:hb + 8] = c
        ct[hb + 32:hb + 40] = c
        st[hb:hb + 8] = -s
        st[hb + 32:hb + 40] = s
    return ct, st


def _run(inp, n_cores=8):
    xp = np.asarray(inp["x_prompt"], np.float32)
    xs = np.asarray(inp["x_sample"], np.float32)
    Bp, Sp, _ = xp.shape
    Bs, Ss, _ = xs.shape
    assert Bp == 2 * n_cores and Bs * 4 == n_cores
    tp = Sp // T
    chunk = Ss // 4
    tsn = chunk // T
    specs = [dict(nc=tp, hl=0, hr=0), dict(nc=tp, hl=0, hr=0), dict(nc=tsn, hl=2, hr=2)]
    nA = [s["nc"] + s["hl"] + s["hr"] for s in specs]
    NTA = sum(nA)
    wts = _prep_weights({k: np.asarray(v, np.float32) for k, v in inp.items() if k not in ("x_prompt", "x_sample")})
    maskT = _mask_strip()
    in_maps = []
    for c in range(n_cores):
        xcols, cs_l, valid_l = [], [], []
        srcs = [(xp[2 * c], 0, Sp), (xp[2 * c + 1], 0, Sp), (xs[c // 4], (c % 4) * chunk, Ss)]
        for s, (src, start, L) in zip(specs, srcs):
            t0 = start - s["hl"] * T
            t1 = start + (s["nc"] + s["hr"]) * T
            pos = np.arange(t0, t1)
            ok = (pos >= 0) & (pos < L)
            seg = np.zeros((t1 - t0, D), np.float32)
            seg[ok] = src[pos[ok]]
            xcols.append(seg.T)
            ct, st = _rope_tables(pos)
            cs_l.append((ct, st))
            valid_l.append(ok.astype(np.float32))
        xTc = np.ascontiguousarray(np.concatenate(xcols, 1))
        ct = np.concatenate([a for a, _ in cs_l], 1).reshape(128, NTA, T)
        st = np.concatenate([b for _, b in cs_l], 1).reshape(128, NTA, T)
        cs = np.ascontiguousarray(np.stack([ct, st], 2).transpose(1, 0, 2, 3)).reshape(NTA, 128, 2 * T)
        valid = np.concatenate(valid_l).reshape(NTA * 4, 128).T
        validx = np.ascontiguousarray(np.repeat(valid[:, :, None], 8, 2)).reshape(128, NTA * 4 * 8)
        m = {"xT": xTc, "cs": cs, "validx": validx.astype(np.float32), "maskT": maskT}
        m.update(wts)
        in_maps.append(m)
    nc = _build(specs)
    res = run_bass_kernel_spmd(nc, in_maps, core_ids=list(range(n_cores)))
    if DEBUG:
        _DBG["res"] = res.results
    yp = np.empty_like(xp)
    ys = np.empty_like(xs)
    for c in range(n_cores):
        y = res.results[c]["yT"]
        yp[2 * c] = y[:, 0:Sp].T
        yp[2 * c + 1] = y[:, Sp:2 * Sp].T
        ys[c // 4, (c % 4) * chunk:(c % 4 + 1) * chunk] = y[:, 2 * Sp:2 * Sp + chunk].T
    return yp, ys


def kernel(**inputs):
    return _run(inputs)
```
